# Optimizing a Trainium2 kernel written in Bass

```python
import math
import jax, jax.numpy as jnp
from jax import lax
import numpy as np

D_MODEL = 1024
BATCH = 8
SEQ = 2048
DEPTH = 1

N_ATTN_HEADS = 8
ATTN_HEAD_DIM = 64
ATTN_V_DIM = 2 * ATTN_HEAD_DIM
QK_WIDTH = N_ATTN_HEADS * 2 * ATTN_HEAD_DIM
ATTN_WIDTH = N_ATTN_HEADS * ATTN_V_DIM
Q_BLOCK = 128
REC_WIDTH = 1024
REC_BLOCKS = 16
REC_BLOCK_DIM = REC_WIDTH // REC_BLOCKS
CONV_WIDTH = 4
RG_LRU_C = 8.0
N_BRANCHES = 2
D_FF = -(-8 * D_MODEL // (3 * 256)) * 256
LN_EPS = 1e-5
ALPHA = (2.0 * DEPTH) ** 0.25
BETA = (8.0 * DEPTH) ** -0.25
Q_OFF = 0
K_OFF = Q_OFF + QK_WIDTH
V_OFF = K_OFF + QK_WIDTH
XR_OFF = V_OFF + ATTN_WIDTH
YR_OFF = XR_OFF + REC_WIDTH
G_OFF = YR_OFF + REC_WIDTH
IN_WIDTH = G_OFF + N_BRANCHES * D_MODEL

kernel_name = "hybrid_diffattn_rglru_deepnorm"


def layer_norm(x, g, b):
    xf = x.astype(jnp.float32)
    mu = jnp.mean(xf, axis=-1, keepdims=True)
    xc = xf - mu
    var = jnp.mean(xc * xc, axis=-1, keepdims=True)
    y = xc * lax.rsqrt(var + LN_EPS) * g.astype(jnp.float32) + b.astype(jnp.float32)
    return y.astype(x.dtype)


def diff_attention(q, k, v, lam):
    B, S = q.shape[0], q.shape[1]
    nblk = S // Q_BLOCK
    scale = ATTN_HEAD_DIM ** -0.5
    slopes = 2.0 ** (-8.0 * jnp.arange(1, N_ATTN_HEADS + 1, dtype=jnp.float32) / N_ATTN_HEADS)
    kpos = jnp.arange(S)
    qb = q.reshape(B, nblk, Q_BLOCK, N_ATTN_HEADS, 2, ATTN_HEAD_DIM).transpose(1, 0, 2, 3, 4, 5)

    def one_block(args):
        blk, q_blk = args
        qpos = blk * Q_BLOCK + jnp.arange(Q_BLOCK)
        dist = (qpos[:, None] - kpos[None, :]).astype(jnp.float32)
        bias = jnp.where(dist[None] >= 0, -slopes[:, None, None] * dist[None], -jnp.inf)
        s = jnp.einsum('bqhcd,bkhcd->bhcqk', q_blk, k,
                       preferred_element_type=jnp.float32) * scale + bias[None, :, None]
        p = jax.nn.softmax(s, axis=-1)
        a = p[:, :, 0] - lam * p[:, :, 1]
        return jnp.einsum('bhqk,bkhe->bqhe', a.astype(v.dtype), v,
                          preferred_element_type=jnp.float32)

    o = lax.map(one_block, (jnp.arange(nblk), qb))
    return o.transpose(1, 0, 2, 3, 4).reshape(B, S, N_ATTN_HEADS, ATTN_V_DIM)


def causal_depthwise_conv(x, w, b):
    y = lax.conv_general_dilated(x, w[:, None, :], window_strides=(1,),
                                 padding=[(CONV_WIDTH - 1, 0)],
                                 dimension_numbers=('NWC', 'WIO', 'NWC'),
                                 feature_group_count=x.shape[-1])
    return y + b


def rg_lru(xr, w_a, b_a, w_i, b_i, lru_lambda):
    B, S, _ = xr.shape
    xb = xr.reshape(B, S, REC_BLOCKS, REC_BLOCK_DIM)
    r = jax.nn.sigmoid((jnp.einsum('bsni,nij->bsnj', xb, w_a).reshape(B, S, REC_WIDTH) + b_a).astype(jnp.float32))
    i = jax.nn.sigmoid((jnp.einsum('bsni,nij->bsnj', xb, w_i).reshape(B, S, REC_WIDTH) + b_i).astype(jnp.float32))
    log_a = -RG_LRU_C * r * jax.nn.softplus(-lru_lambda.astype(jnp.float32))
    a = jnp.exp(log_a)
    u = jnp.sqrt(-jnp.expm1(2.0 * log_a)) * (i * xr.astype(jnp.float32))

    def combine(left, right):
        a_l, h_l = left
        a_r, h_r = right
        return a_l * a_r, a_r * h_l + h_r

    _, h = lax.associative_scan(combine, (a, u), axis=1)
    return h


def hybrid_mixer(x, w_in, b_gate, lambda_q1, lambda_k1, lambda_q2, lambda_k2, subln_g,
                 lam_init, conv_w, conv_b, w_a, b_a, w_i, b_i, lru_lambda,
                 w_br_attn, w_br_rec, w_out):
    B, S, _ = x.shape
    z = x @ w_in
    q, k, v, xr, yr, gl = jnp.split(z, [K_OFF, V_OFF, XR_OFF, YR_OFF, G_OFF], axis=-1)
    q = q.reshape(B, S, N_ATTN_HEADS, 2, ATTN_HEAD_DIM)
    k = k.reshape(B, S, N_ATTN_HEADS, 2, ATTN_HEAD_DIM)
    v = v.reshape(B, S, N_ATTN_HEADS, ATTN_V_DIM)
    lam = (jnp.exp(jnp.sum(lambda_q1.astype(jnp.float32) * lambda_k1.astype(jnp.float32)))
           - jnp.exp(jnp.sum(lambda_q2.astype(jnp.float32) * lambda_k2.astype(jnp.float32)))
           + lam_init)
    o = diff_attention(q, k, v, lam)
    o = o * lax.rsqrt(jnp.mean(o * o, axis=-1, keepdims=True) + LN_EPS)
    o = o * subln_g.astype(jnp.float32) * (1.0 - lam_init)
    attn_out = o.reshape(B, S, ATTN_WIDTH).astype(x.dtype)
    xc = causal_depthwise_conv(xr, conv_w, conv_b)
    h = rg_lru(xc, w_a, b_a, w_i, b_i, lru_lambda)
    rec_out = (h * jax.nn.gelu(yr.astype(jnp.float32))).astype(x.dtype)
    g = jax.nn.sigmoid((gl + b_gate).astype(jnp.float32)).reshape(B, S, N_BRANCHES, D_MODEL).astype(x.dtype)
    merged = g[:, :, 0] * (attn_out @ w_br_attn) + g[:, :, 1] * (rec_out @ w_br_rec)
    return merged @ w_out


def swiglu(x, w_gate, w_up, w_down):
    return (jax.nn.silu(x @ w_gate) * (x @ w_up)) @ w_down


def setup_inputs(seed: int = 0) -> dict:
    key = jax.random.key(seed)
    ks = jax.random.split(key, 26)
    f32 = jnp.float32
    nrm = lambda k, shape, s: jax.random.normal(k, shape, f32) * s
    col_scale = jnp.ones((IN_WIDTH,), f32).at[V_OFF:V_OFF + ATTN_WIDTH].set(BETA)
    u = jax.random.uniform(ks[14], (DEPTH, REC_WIDTH), f32, minval=0.9, maxval=0.999)
    a0 = u ** (1.0 / RG_LRU_C)
    lru_lambda = jnp.log(a0) - jnp.log1p(-a0)
    return {
        "x": nrm(ks[0], (BATCH, SEQ, D_MODEL), 1.0),
        "w_in": nrm(ks[1], (DEPTH, D_MODEL, IN_WIDTH), D_MODEL ** -0.5) * col_scale,
        "b_gate": nrm(ks[2], (DEPTH, N_BRANCHES * D_MODEL), 0.01),
        "lambda_q1": nrm(ks[3], (DEPTH, ATTN_HEAD_DIM), 0.1),
        "lambda_k1": nrm(ks[4], (DEPTH, ATTN_HEAD_DIM), 0.1),
        "lambda_q2": nrm(ks[5], (DEPTH, ATTN_HEAD_DIM), 0.1),
        "lambda_k2": nrm(ks[6], (DEPTH, ATTN_HEAD_DIM), 0.1),
        "subln_g": 1.0 + nrm(ks[7], (DEPTH, ATTN_V_DIM), 0.02),
        "conv_w": nrm(ks[8], (DEPTH, CONV_WIDTH, REC_WIDTH), CONV_WIDTH ** -0.5),
        "conv_b": nrm(ks[9], (DEPTH, REC_WIDTH), 0.01),
        "w_a": nrm(ks[10], (DEPTH, REC_BLOCKS, REC_BLOCK_DIM, REC_BLOCK_DIM), REC_BLOCK_DIM ** -0.5),
        "b_a": nrm(ks[11], (DEPTH, REC_WIDTH), 0.01),
        "w_i": nrm(ks[12], (DEPTH, REC_BLOCKS, REC_BLOCK_DIM, REC_BLOCK_DIM), REC_BLOCK_DIM ** -0.5),
        "b_i": nrm(ks[13], (DEPTH, REC_WIDTH), 0.01),
        "lru_lambda": lru_lambda,
        "w_br_attn": nrm(ks[15], (DEPTH, ATTN_WIDTH, D_MODEL), ATTN_WIDTH ** -0.5 * BETA),
        "w_br_rec": nrm(ks[16], (DEPTH, REC_WIDTH, D_MODEL), REC_WIDTH ** -0.5 * BETA),
        "w_out": nrm(ks[17], (DEPTH, D_MODEL, D_MODEL), D_MODEL ** -0.5 * BETA),
        "ln1_g": 1.0 + nrm(ks[18], (DEPTH, D_MODEL), 0.02),
        "ln1_b": nrm(ks[19], (DEPTH, D_MODEL), 0.01),
        "w_gate": nrm(ks[20], (DEPTH, D_MODEL, D_FF), D_MODEL ** -0.5 * BETA),
        "w_up": nrm(ks[21], (DEPTH, D_MODEL, D_FF), D_MODEL ** -0.5 * BETA),
        "w_down": nrm(ks[22], (DEPTH, D_FF, D_MODEL), D_FF ** -0.5 * BETA),
        "ln2_g": 1.0 + nrm(ks[23], (DEPTH, D_MODEL), 0.02),
        "ln2_b": nrm(ks[24], (DEPTH, D_MODEL), 0.01),
    }


def reference(x, w_in, b_gate, lambda_q1, lambda_k1, lambda_q2, lambda_k2, subln_g,
              conv_w, conv_b, w_a, b_a, w_i, b_i, lru_lambda, w_br_attn, w_br_rec, w_out,
              ln1_g, ln1_b, w_gate, w_up, w_down, ln2_g, ln2_b):
    for l in range(DEPTH):
        lam_init = 0.8 - 0.6 * math.exp(-0.3 * l)
        mix = hybrid_mixer(x, w_in[l], b_gate[l], lambda_q1[l], lambda_k1[l], lambda_q2[l],
                           lambda_k2[l], subln_g[l], lam_init, conv_w[l], conv_b[l],
                           w_a[l], b_a[l], w_i[l], b_i[l], lru_lambda[l],
                           w_br_attn[l], w_br_rec[l], w_out[l])
        x = layer_norm(ALPHA * x + mix, ln1_g[l], ln1_b[l])
        x = layer_norm(ALPHA * x + swiglu(x, w_gate[l], w_up[l], w_down[l]), ln2_g[l], ln2_b[l])
    return x
```

```python
import math
from contextlib import ExitStack

import numpy as np
import concourse.bass as bass
import concourse.mybir as mybir
from concourse.bass_utils import run_bass_kernel_spmd

F32 = mybir.dt.float32
BF16 = mybir.dt.bfloat16
AF = mybir.ActivationFunctionType
ALU = mybir.AluOpType
AX = mybir.AxisListType

ENGS = ['pe', 'act', 'dve', 'pool', 'sp']
ENGATTR = {'pe': 'tensor', 'act': 'scalar', 'dve': 'vector', 'pool': 'gpsimd', 'sp': 'sync'}

SAME_ENGINE_SYNC = True

D = 1024
SEQ = 2048
NH = 8
HD = 64
DFF = 2816
NF = DFF // 128
Q_OFF, K_OFF, V_OFF, XR_OFF, YR_OFF, G_OFF = 0, 1024, 2048, 3072, 4096, 5120
IN_W = 7168
LN_EPS = 1e-5
ALPHA = 2.0 ** 0.25
LAM_INIT = 0.8 - 0.6 * math.exp(0.0)
SCALE = HD ** -0.5
NTB = SEQ // 128
NTT = SEQ // 512


class Op:
    __slots__ = ('eng', 'fn', 'deps', 'slot', 'sem', 'val', 'needs_inc', 'waits', 'seq')


class Sched:
    def __init__(self, same_engine_sync=True):
        self.ops = {e: [] for e in ENGS}
        self.last_w = {}
        self.readers = {}
        self.same = same_engine_sync
        self.seq = 0

    def add(self, eng, fn, reads=(), writes=(), slot=None):
        op = Op()
        op.eng = eng
        op.fn = fn
        op.slot = slot
        op.needs_inc = slot is not None
        self.seq += 1
        op.seq = self.seq
        deps = {}
        for k in reads:
            w = self.last_w.get(k)
            if w is not None:
                deps[id(w)] = w
        for k in writes:
            w = self.last_w.get(k)
            if w is not None:
                deps[id(w)] = w
            for r in self.readers.get(k, ()):
                deps[id(r)] = r
        op.deps = list(deps.values())
        for k in reads:
            lst = self.readers.setdefault(k, [])
            if op.slot is None:
                lst[:] = [r for r in lst if not (r.slot is None and r.eng == eng)]
            lst.append(op)
        for k in writes:
            self.last_w[k] = op
            self.readers[k] = []
        self.ops[eng].append(op)
        return op

    def alias(self, old_keys, new_keys):
        acc = {}
        for k in old_keys:
            w = self.last_w.get(k)
            if w is not None:
                acc[id(w)] = w
            for r in self.readers.get(k, ()):
                acc[id(r)] = r
        best = {}
        keep = []
        for o in acc.values():
            if o.slot is not None:
                keep.append(o)
            else:
                b = best.get(o.eng)
                if b is None or b.seq < o.seq:
                    best[o.eng] = o
        keep.extend(best.values())
        for k in new_keys:
            self.last_w.pop(k, None)
            self.readers[k] = list(keep)

    def _sync(self, op, d):
        if d.slot is not None:
            return True
        if d.eng != op.eng:
            return True
        if op.slot is not None:
            return True
        if d.eng == 'pe':
            return False
        return self.same

    def finalize(self, nc, stack):
        for e in ENGS:
            for op in self.ops[e]:
                for d in op.deps:
                    if d.slot is None and self._sync(op, d):
                        d.needs_inc = True
        self.esem = {e: stack.enter_context(nc.semaphore('s_' + e)) for e in ENGS}
        slotsem = {}
        slotcnt = {}
        allops = sorted((op for e in ENGS for op in self.ops[e]), key=lambda o: o.seq)
        for op in allops:
            if op.slot is not None:
                if op.slot not in slotsem:
                    slotsem[op.slot] = stack.enter_context(nc.semaphore('d_%d' % len(slotsem)))
                    slotcnt[op.slot] = 0
                slotcnt[op.slot] += 1
                op.sem = slotsem[op.slot]
                op.val = 16 * slotcnt[op.slot]
        for e in ENGS:
            tick = 0
            for op in self.ops[e]:
                if op.slot is None and op.needs_inc:
                    tick += 1
                    op.sem = self.esem[e]
                    op.val = tick
        self.nsem = len(slotsem) + len(ENGS)
        for e in ENGS:
            waited = {}
            for op in self.ops[e]:
                w = {}
                for d in op.deps:
                    if not self._sync(op, d):
                        continue
                    key = id(d.sem)
                    if key not in w or w[key][1] < d.val:
                        w[key] = (d.sem, d.val)
                op.waits = []
                for key, (sem, val) in w.items():
                    if waited.get(key, 0) >= val:
                        continue
                    waited[key] = val
                    op.waits.append((sem, val))

    def emit(self, nc):
        with nc.Block() as block:
            for e in ENGS:
                ops = self.ops[e]
                if not ops:
                    continue

                def body(eng, ops=ops):
                    for op in ops:
                        for (sem, val) in op.waits:
                            eng.wait_ge(sem, val)
                        if op.fn is None:
                            continue
                        ins = op.fn(eng)
                        if op.slot is not None:
                            ins.then_inc(op.sem, 16)
                        elif op.needs_inc:
                            ins.then_inc(op.sem, 1)

                getattr(block, ENGATTR[e])(body)


CP_ALIBI = 0
CP_BGATE = 160
CP_CONVW = 176
CP_CONVB = 208
CP_BA = 216
CP_BI = 224
CP_LRU = 232
CP_GSUB = 240
CP_LAMV = 368
CP_W = 624
BP_ID = 0
BP_TRI = 128
BP_WA = 256
BP_WI = 1280
BP_MASKB = 2304
BP_W = 2432


def build_nc(dbg=None, stop_after=None):
    nc = bass.Bass("TRN2", target_bir_lowering=False)

    def din(name, shape):
        return nc.dram_tensor(name, shape, F32, kind="ExternalInput").ap()

    x_d = din("x", [SEQ, D])
    w_in_d = din("w_in", [D, IN_W])
    w_bra_d = din("w_br_attn", [D, D])
    w_brr_d = din("w_br_rec", [D, D])
    w_out_d = din("w_out", [D, D])
    w_gate_d = din("w_gate", [D, DFF])
    w_up_d = din("w_up", [D, DFF])
    w_down_d = din("w_down", [DFF, D])
    cpack_d = din("cpack", [128, CP_W])
    bpack_d = din("bpack", [128, BP_W])
    lnp_d = din("lnp", [4, 128, D])
    out_d = nc.dram_tensor("out", [SEQ, D], F32, kind="ExternalOutput").ap()
    dbg_d = None
    if dbg is not None:
        dbg_d = nc.dram_tensor("dbg", [128, dbg[1]], dbg[2], kind="ExternalOutput").ap()

    S = Sched(same_engine_sync=SAME_ENGINE_SYNC)
    A = S.add

    def finish(out_keys):
        fin_reads = list(out_keys)
        if dbg is not None:
            fin_reads.append('dbg')
        A('sp', None, reads=fin_reads)
        stack = ExitStack()
        S.finalize(nc, stack)
        S.emit(nc)
        stack.close()
        nc._sched_stats = {e: len(S.ops[e]) for e in ENGS}
        return nc

    SB_BASE, SB_TOP = 16512, 229344
    off = [SB_BASE]

    def alloc(name, shape, dt, at=None):
        nbytes = int(np.prod(shape[1:])) * (4 if dt == F32 else 2)
        nbytes = (nbytes + 31) // 32 * 32
        if at is None:
            o = off[0]
            off[0] += nbytes
        else:
            o = at
        t = nc.alloc_sbuf_tensor_at(name, list(shape), dt, offset=o)
        return t

    cp = alloc("cp", [128, CP_W], F32)
    bp = alloc("bp", [128, BP_W], BF16)
    sc = alloc("sc", [128, 96], F32)
    lng_off = off[0]
    lng = alloc("lng", [128, D], F32)
    lnb = alloc("lnb", [128, D], F32)
    big0 = off[0]
    xT = alloc("xT", [128, 8, SEQ], BF16)
    attnT = alloc("attnT", [128, 8, SEQ], BF16)
    recT = alloc("recT", [128, 8, SEQ], BF16)
    x1 = alloc("x1", [128, NTB, D], F32, at=big0)
    x1T = alloc("x1T", [128, 8, SEQ], BF16, at=big0 + 65536)
    ring0 = off[0]
    NRING = 8
    ring = [alloc("ring%d" % i, [128, 8, 128], BF16) for i in range(NRING)]
    ringbig = [alloc("ringbig%d" % i, [128, 8, 512], BF16, at=ring0 + i * 8192) for i in range(2)]
    work0 = off[0]
    WORK = SB_TOP - work0
    assert WORK >= 80 * 1024 - 1024, WORK

    def walloc_reset():
        off[0] = work0

    def walloc(name, shape, dt):
        t = alloc(name, shape, dt)
        assert off[0] <= SB_TOP, (name, off[0])
        return t

    psS = [nc.alloc_psum_tensor("psS%d" % i, [128, 1024], F32) for i in range(2)]
    PS = [psS[0][:, 0:512], psS[0][:, 512:1024], psS[1][:, 0:512], psS[1][:, 512:1024]] + \
        [nc.alloc_psum_tensor("ps%d" % i, [128, 512], F32) for i in range(4, 8)]

    def psk(i):
        return ('ps', i)

    ident = bp[:, BP_ID:BP_ID + 128]
    tri = bp[:, BP_TRI:BP_TRI + 128]
    maskb = bp[:, BP_MASKB:BP_MASKB + 128]

    A('sp', lambda e: e.dma_start(out=cp[:], in_=cpack_d), writes=['cp'], slot='cp')
    A('pool', lambda e: e.dma_start(out=bp[:], in_=bpack_d), writes=['bp'], slot='bp')

    SC_NEGLAM, SC_S1, SC_S2, SC_E1, SC_E2 = 0, 1, 2, 3, 4
    SC_CM8, SC_C16, SC_C8 = 8, 16, 24
    SC_TMP = 32

    walloc_reset()
    lamtmp = walloc("lamtmp", [128, 2, 64], F32)
    lamv = cp[:, CP_LAMV:CP_LAMV + 256].rearrange("p (a n) -> p a n", a=4)
    A('dve', lambda e: e.tensor_tensor(out=lamtmp[:, 0, :], in0=lamv[:, 0, :], in1=lamv[:, 1, :], op=ALU.mult),
      reads=['cp'], writes=['lamtmp0'])
    A('dve', lambda e: e.tensor_tensor(out=lamtmp[:, 1, :], in0=lamv[:, 2, :], in1=lamv[:, 3, :], op=ALU.mult),
      reads=['cp'], writes=['lamtmp1'])
    A('dve', lambda e: e.reduce_sum(out=sc[:, SC_S1:SC_S1 + 1], in_=lamtmp[:, 0, :], axis=AX.X),
      reads=['lamtmp0'], writes=['sc_s1'])
    A('dve', lambda e: e.reduce_sum(out=sc[:, SC_S2:SC_S2 + 1], in_=lamtmp[:, 1, :], axis=AX.X),
      reads=['lamtmp1'], writes=['sc_s2'])
    A('act', lambda e: e.activation(out=sc[:, SC_E1:SC_E1 + 1], in_=sc[:, SC_S1:SC_S1 + 1], func=AF.Exp),
      reads=['sc_s1'], writes=['sc_e1'])
    A('act', lambda e: e.activation(out=sc[:, SC_E2:SC_E2 + 1], in_=sc[:, SC_S2:SC_S2 + 1], func=AF.Exp),
      reads=['sc_s2'], writes=['sc_e2'])
    A('dve', lambda e: e.tensor_tensor(out=sc[:, SC_TMP:SC_TMP + 1], in0=sc[:, SC_E2:SC_E2 + 1],
                                       in1=sc[:, SC_E1:SC_E1 + 1], op=ALU.subtract),
      reads=['sc_e1', 'sc_e2'], writes=['sc_tmp'])
    A('dve', lambda e: e.tensor_scalar(out=sc[:, SC_NEGLAM:SC_NEGLAM + 1], in0=sc[:, SC_TMP:SC_TMP + 1],
                                       scalar1=-LAM_INIT, scalar2=None, op0=ALU.add),
      reads=['sc_tmp'], writes=['neglam'])
    gsub = cp[:, CP_GSUB:CP_GSUB + 128]
    A('dve', lambda e: e.tensor_scalar(out=gsub, in0=gsub, scalar1=1.0 - LAM_INIT, scalar2=None, op0=ALU.mult),
      reads=['cp'], writes=['gsub'])
    A('act', lambda e: e.activation(out=sc[:, SC_TMP + 8:SC_TMP + 16], in_=cp[:, CP_LRU:CP_LRU + 8], func=AF.Exp,
                                    scale=-1.0), reads=['cp'], writes=['sp_z'])
    A('act', lambda e: e.activation(out=sc[:, SC_TMP + 16:SC_TMP + 24], in_=sc[:, SC_TMP + 8:SC_TMP + 16],
                                    func=AF.Ln, bias=1.0), reads=['sp_z'], writes=['sp_l'])
    for (col, mul) in ((SC_CM8, -4.0), (SC_C16, -8.0), (SC_C8, 4.0)):
        A('dve', lambda e, col=col, mul=mul: e.tensor_scalar(out=sc[:, col:col + 8],
                                                             in0=sc[:, SC_TMP + 16:SC_TMP + 24],
                                                             scalar1=mul, scalar2=None, op0=ALU.mult),
          reads=['sp_l'], writes=['sc_c%d' % col])
    CKEYS = ['sc_c%d' % c for c in (SC_CM8, SC_C16, SC_C8)]
    SC_HBA, SC_HBI = 64, 72
    A('dve', lambda e: e.tensor_scalar(out=sc[:, SC_HBA:SC_HBA + 8], in0=cp[:, CP_BA:CP_BA + 8], scalar1=0.5,
                                       scalar2=None, op0=ALU.mult), reads=['cp'], writes=['sc_hba'])
    A('dve', lambda e: e.tensor_scalar(out=sc[:, SC_HBI:SC_HBI + 8], in0=cp[:, CP_BI:CP_BI + 8], scalar1=0.5,
                                       scalar2=None, op0=ALU.mult), reads=['cp'], writes=['sc_hbi'])

    ring_i = [0]

    def load_w(src2d):
        s = ring_i[0] % NRING
        ring_i[0] += 1
        t = ring[s]
        A('pool', lambda e: e.dma_start(out=t[:], in_=src2d.rearrange("(c p) n -> p c n", p=128)),
          writes=[('ring', s)], slot=('ring', s))
        return t, ('ring', s)

    ps_i = [0]

    def next_ps(banks):
        b = banks[ps_i[0] % len(banks)]
        ps_i[0] += 1
        return b

    def xT_keys(tt):
        return [('xT', 4 * tt + i) for i in range(4)]

    NXB = 4
    xb = [walloc("xb%d" % i, [128, D], BF16) for i in range(NXB)]
    def x_load(t):
        b = xb[t % NXB]
        bk = ('xb', t % NXB)
        A('pool', lambda e: e.dma_start(out=b[:], in_=x_d[t * 128:(t + 1) * 128, :]), writes=[bk], slot=bk)

    def x_block(t):
        b = xb[t % NXB]
        bk = ('xb', t % NXB)
        pb = 6 + (t % 2)
        pv = PS[pb][:].bitcast(BF16).rearrange("p (c n) -> p c n", c=8)

        def tr(e):
            for c in range(8):
                ins = e.transpose(pv[:, c, :], b[:, c * 128:(c + 1) * 128], ident)
            return ins
        A('pe', tr, reads=[bk, 'bp'], writes=[psk(pb)])
        A('dve', lambda e: e.tensor_copy(out=xT[:, :, t * 128:(t + 1) * 128], in_=pv),
          reads=[psk(pb)], writes=[('xT', t)])

    v_sb = walloc("v_sb", [128, NTB, NH, 130], BF16)
    qT = [walloc("qT%d" % i, [128, SEQ], BF16) for i in range(2)]
    kT = [walloc("kT%d" % i, [128, SEQ], BF16) for i in range(2)]
    PT = [walloc("PT%d" % i, [128, 2, 512], BF16) for i in range(2)]
    attn_epi_start = off[0]
    oraw = [walloc("oraw%d" % i, [128, 8, 130], F32) for i in range(2)]
    ojunk = walloc("ojunk", [128, 4, 128], F32)
    atok = [walloc("atok%d" % i, [128, 4, 128], BF16) for i in range(2)]
    asc = [walloc("asc%d" % i, [128, 16], F32) for i in range(2)]
    attn_core_keys = ['v_ones'] + [('v', t, n) for t in range(NTB) for n in range(2)] + \
        [('qT', i, t) for i in range(2) for t in range(NTT)] + [('kT', i, t) for i in range(2) for t in range(NTT)] + \
        [('PT', i, q) for i in range(2) for q in range(4)] + \
        [('xb', i) for i in range(4)] + ['lamtmp0', 'lamtmp1']
    attn_epi_keys = ['ojunk'] + [(nm, i) for nm in ('atok', 'rinv', 'ss') for i in range(2)] + \
        [('oraw', i, b_) for i in range(2) for b_ in range(3)]
    attn_epi_off = None
    attn_work_keys = attn_core_keys + attn_epi_keys

    A('dve', lambda e: e.memset(v_sb[:, :, :, 128:130], 1.0), writes=['v_ones'])
    def v_block(t):
        for n in range(2):
            wt = ringbig[n]
            wkeys = [('ring', 4 * n + i) for i in range(4)]
            pb = next_ps([0, 1, 2, 3, 4, 5])

            def mm(e, wt=wt, pb=pb):
                for kc in range(8):
                    ins = e.matmul(PS[pb][:], xT[:, kc, t * 128:(t + 1) * 128], wt[:, kc, :],
                                   start=(kc == 0), stop=(kc == 7))
                return ins
            A('pe', mm, reads=wkeys + [('xT', t)], writes=[psk(pb)])
            A('dve', lambda e, n=n, pb=pb: e.tensor_copy(
                out=v_sb[:, t, 4 * n:4 * n + 4, 0:128], in_=PS[pb][:].rearrange("p (h d) -> p h d", h=4)),
              reads=[psk(pb)], writes=[('v', t, n)])

    for t in range(NXB):
        x_load(t)
    for n in range(2):
        A('pool', lambda e, n=n: e.dma_start(
            out=ringbig[n][:],
            in_=w_in_d[:, V_OFF + n * 512:V_OFF + (n + 1) * 512].rearrange("(c p) n -> p c n", p=128)),
          writes=[('ring', 4 * n + i) for i in range(4)], slot=('ring', 4 * n))
    for t in range(NTB):
        x_block(t)
        if t + NXB < NTB:
            x_load(t + NXB)
        if t >= 1:
            v_block(t - 1)
    v_block(NTB - 1)
    ring_i[0] = 0

    if dbg is not None and dbg[0] == 'xT':
        A('sp', lambda e: e.dma_start(out=dbg_d, in_=xT[:].rearrange("p c n -> p (c n)")),
          reads=[('xT', t) for t in range(NTB)], writes=['dbg'], slot='dbg')
    if stop_after == 'xT':
        return finish([])

    def proj_steps(h):
        hb = h % 2
        steps = []
        for which, base, dst, kname in (('q', Q_OFF, qT[hb], 'qT'), ('k', K_OFF, kT[hb], 'kT')):
            holder = {}

            def ld(base=base, holder=holder):
                holder['w'] = load_w(w_in_d[:, base + h * 128: base + (h + 1) * 128])
            for tt in range(NTT):
                def step(tt=tt, dst=dst, kname=kname, holder=holder, ld=ld):
                    if tt == 0:
                        ld()
                    wt, wk = holder['w']
                    pb = 7

                    def mm(e):
                        for kc in range(8):
                            ins = e.matmul(PS[pb][:], wt[:, kc, :], xT[:, kc, tt * 512:(tt + 1) * 512],
                                           start=(kc == 0), stop=(kc == 7))
                        return ins
                    A('pe', mm, reads=[wk] + xT_keys(tt), writes=[psk(pb)])
                    A('act', lambda e: e.copy(out=dst[:, tt * 512:(tt + 1) * 512], in_=PS[pb][:]),
                      reads=[psk(pb)], writes=[(kname, hb, tt)])
                steps.append(step)
        return steps

    def proj_substeps(h):
        hb = h % 2
        subs = []
        for which, base, dst, kname in (('q', Q_OFF, qT[hb], 'qT'), ('k', K_OFF, kT[hb], 'kT')):
            holder = {}
            for tt in range(NTT):
                for part in range(4):
                    def sub(tt=tt, part=part, dst=dst, kname=kname, holder=holder, base=base):
                        if tt == 0 and part == 0:
                            holder['w'] = load_w(w_in_d[:, base + h * 128: base + (h + 1) * 128])
                        wt, wk = holder['w']

                        def mm(e):
                            for kc in (2 * part, 2 * part + 1):
                                ins = e.matmul(PS[7][:], wt[:, kc, :], xT[:, kc, tt * 512:(tt + 1) * 512],
                                               start=(kc == 0), stop=(kc == 7))
                            return ins
                        A('pe', mm, reads=[wk] + xT_keys(tt), writes=[psk(7)])
                        if part == 3:
                            A('act', lambda e: e.copy(out=dst[:, tt * 512:(tt + 1) * 512], in_=PS[7][:]),
                              reads=[psk(7)], writes=[(kname, hb, tt)])
                    subs.append(sub)
        return subs

    pq = []
    pq_done = [0]

    def emit_proj(n):
        for _ in range(n):
            if pq:
                pq.pop(0)()
                pq_done[0] += 1

    def finish_proj_group():
        while pq and pq_done[0] % 4 != 0:
            emit_proj(1)

    epi_i = [0]
    deferred = []

    def tick():
        for d_ in deferred:
            d_[0] -= 1
        while deferred and deferred[0][0] <= 0:
            deferred.pop(0)[1]()

    def flush_deferred():
        while deferred:
            deferred.pop(0)[1]()

    steps_left = [0]

    def attn_head(h, interleave):
        hb = h % 2
        steps_left[0] = 40
        W = 256 if h == 0 else 512
        for qt in range(NTT):
            q0 = qt * 512
            nkb = 4 * qt + 4
            par = epi_i[0] % 2
            epi_i[0] += 1
            pend = None
            for kb in range(nkb):
                sl = kb % 2
                i0 = max(0, kb - 4 * qt)
                c0 = i0 * 128
                skeys = [psk(2 * sl), psk(2 * sl + 1)]

                diag = kb >= 4 * qt

                def smm(e, sl=sl, kb=kb, c0=c0, q0=q0, diag=diag):
                    for c in range(2):
                        ins = e.matmul(psS[sl][:, c * 512 + c0:(c + 1) * 512],
                                       kT[hb][64 * c:64 * c + 64, kb * 128:(kb + 1) * 128],
                                       qT[hb][64 * c:64 * c + 64, q0 + c0:q0 + 512],
                                       start=True, stop=not diag, skip_group_check=True)
                    if diag:
                        for c in range(2):
                            ins = e.matmul(psS[sl][:, c * 512 + c0:c * 512 + c0 + 128], ident, maskb,
                                           start=False, stop=True, skip_group_check=True)
                    return ins
                A('pe', smm, reads=[('kT', hb, kb // 4), ('qT', hb, qt), 'bp'], writes=skeys)
                sv = psS[sl][:].rearrange("p (c n) -> p c n", c=2)
                cs = c0
                while cs < 512:
                    ce = min(512, (cs // W + 1) * W)
                    j = (q0 + (cs // W) * W + W // 2 - kb * 128) // 128
                    assert -3 <= j <= 16
                    bias = cp[:, CP_ALIBI + h * 20 + j + 3: CP_ALIBI + h * 20 + j + 4]
                    A('act', lambda e, sl=sl, cs=cs, ce=ce, bias=bias, sv=sv: e.activation(
                        out=PT[sl][:, :, cs:ce], in_=sv[:, :, cs:ce], func=AF.Exp, bias=bias, scale=SCALE),
                      reads=skeys + ['cp'], writes=[('PT', sl, qi_) for qi_ in range(cs // 128, ce // 128)])
                    cs = ce
                if pend is not None:
                    pend()

                def pv(kb=kb, sl=sl, i0=i0, qt=qt):
                    def f(e):
                        for c in range(2):
                            for qi in range(i0, 4):
                                qb = 4 * qt + qi
                                a = c * 4 + qi
                                acc = PS[4 + a // 3][:, (a % 3) * 130:(a % 3) * 130 + 129]
                                ins = e.matmul(acc, PT[sl][:, c, qi * 128:(qi + 1) * 128], v_sb[:, kb, h, 0:129],
                                               start=(kb == 0 and a % 3 == 0), stop=(kb == qb),
                                               skip_group_check=True)
                        return ins
                    A('pe', f, reads=[('PT', sl, qi_) for qi_ in range(i0, 4)] + [('v', kb, h // 4), 'v_ones'],
                      writes=[psk(4), psk(5), psk(6)])
                pend = pv
                tick()
                steps_left[0] -= 1
                emit_proj(-(-len(pq) // max(1, steps_left[0])) if pq else 0)
            pend()
            orw = oraw[par]
            while any(d_[2] == par for d_ in deferred):
                deferred.pop(0)[1]()
            for b_ in range(3):
                na = 3 if b_ < 2 else 2
                A('dve', lambda e, b_=b_, na=na, orw=orw: e.tensor_copy(
                    out=orw[:, 3 * b_:3 * b_ + na, :].rearrange("p a n -> p (a n)"), in_=PS[4 + b_][:, 0:130 * na]),
                  reads=[psk(4 + b_)], writes=[('oraw', par, b_)])
            okeys = [('oraw', par, b_) for b_ in range(3)]
            rinv = asc[par][:, 0:8]
            ss = asc[par][:, 8:12]
            A('dve', lambda e, orw=orw, rinv=rinv: e.reciprocal(out=rinv.unsqueeze(2), in_=orw[:, :, 128:129]),
              reads=okeys, writes=[('rinv', par)])
            A('dve', lambda e, rinv=rinv: e.tensor_scalar(out=rinv[:, 4:8], in0=rinv[:, 4:8],
                                                          scalar1=sc[:, SC_NEGLAM:SC_NEGLAM + 1], scalar2=None,
                                                          op0=ALU.mult),
              reads=[('rinv', par), 'neglam'], writes=[('rinv', par)])
            A('dve', lambda e, orw=orw, rinv=rinv: e.tensor_tensor(
                out=orw[:, :, 0:128], in0=orw[:, :, 0:128], in1=rinv.unsqueeze(2).broadcast_to([128, 8, 128]),
                op=ALU.mult), reads=okeys + [('rinv', par)], writes=okeys)
            A('dve', lambda e, orw=orw: e.tensor_tensor(out=orw[:, 0:4, 0:128], in0=orw[:, 0:4, 0:128],
                                                        in1=orw[:, 4:8, 0:128], op=ALU.add),
              reads=okeys, writes=okeys)
            A('dve', lambda e, orw=orw: e.tensor_tensor(out=ojunk[:], in0=orw[:, 0:4, 0:128], in1=orw[:, 0:4, 0:128],
                                                        op=ALU.mult), reads=okeys, writes=['ojunk'])
            A('dve', lambda e, ss=ss: e.reduce_sum(out=ss, in_=ojunk[:], axis=AX.X), reads=['ojunk'],
              writes=[('ss', par)])
            atk = atok[par]

            def stage1(ss=ss, orw=orw, atk=atk, okeys=okeys, par=par):
                A('act', lambda e: e.activation(out=ss, in_=ss, func=AF.Ln, bias=LN_EPS, scale=1.0 / 128),
                  reads=[('ss', par)], writes=[('ss', par)])
                A('act', lambda e: e.activation(out=ss, in_=ss, func=AF.Exp, scale=-0.5),
                  reads=[('ss', par)], writes=[('ss', par)])
                A('dve', lambda e: e.tensor_tensor(
                    out=orw[:, 0:4, 0:128], in0=orw[:, 0:4, 0:128], in1=ss.unsqueeze(2).broadcast_to([128, 4, 128]),
                    op=ALU.mult), reads=okeys + [('ss', par)], writes=okeys)
                A('dve', lambda e: e.tensor_tensor(
                    out=atk[:], in0=orw[:, 0:4, 0:128], in1=gsub.unsqueeze(1).broadcast_to([128, 4, 128]),
                    op=ALU.mult), reads=okeys + ['gsub'], writes=[('atok', par)])

            def stage2(atk=atk, par=par, q0=q0, h=h, qt=qt):
                pb = 7
                finish_proj_group()
                pv_ = PS[pb][:].bitcast(BF16)[:, 0:512].rearrange("p (c n) -> p c n", c=4)

                def tr(e):
                    for qi in range(4):
                        ins = e.transpose(pv_[:, qi, :], atk[:, qi, :], ident)
                    return ins
                A('pe', tr, reads=[('atok', par), 'bp'], writes=[psk(pb)])
                A('act', lambda e: e.copy(out=attnT[:, h, q0:q0 + 512], in_=pv_.rearrange("p c n -> p (c n)")),
                  reads=[psk(pb)], writes=[('attnT', h, qt)])
            deferred.append([7, stage1, par])
            deferred.append([10, stage2, par])

    st0 = proj_steps(0)
    for s_ in st0:
        s_()
    for h in range(NH):
        pq_done[0] = 0
        pq.extend(proj_substeps(h + 1) if h + 1 < NH else [])
        attn_head(h, None)
        emit_proj(len(pq))
    flush_deferred()

    if dbg is not None and dbg[0] == 'attnT':
        A('sp', lambda e: e.dma_start(out=dbg_d, in_=attnT[:].rearrange("p c n -> p (c n)")),
          reads=[('attnT', h, qt) for h in range(NH) for qt in range(NTT)], writes=['dbg'], slot='dbg')
    if stop_after == 'attnT':
        return finish([])

    walloc_reset()
    HT = SEQ // 2
    P0 = 8
    xrp = [walloc("xrp%d" % i, [128, P0 + SEQ], F32) for i in range(2)]
    U = []
    for i in range(2):
        U.append(dict(xc=walloc("xc%d" % i, [128, HT], F32), rr=walloc("rr%d" % i, [128, HT], F32),
                      ii=walloc("ii%d" % i, [128, HT], F32), a2=walloc("a2%d" % i, [128, HT], F32),
                      tt=walloc("tt%d" % i, [128, HT], F32), xcb=walloc("xcb%d" % i, [128, HT], BF16)))
    hcar = walloc("hcar", [128, 16], F32)
    rec_keys = [('xrp_pad', i) for i in range(2)] + [('xrp', i, t) for i in range(2) for t in range(NTT)] + \
        [(nm, i) for nm in ('xc', 'xcb', 'a2') for i in range(2)] + \
        [(nm, i, t) for nm in ('rr', 'ii', 'tt') for i in range(2) for t in range(2)] + ['hcar']
    assert off[0] <= attn_epi_start, (off[0], attn_epi_start)
    S.alias(attn_core_keys, rec_keys)
    wa_bd = bp[:, BP_WA:BP_WA + 1024].rearrange("p (m n) -> p m n", m=8)
    wi_bd = bp[:, BP_WI:BP_WI + 1024].rearrange("p (m n) -> p m n", m=8)
    for i in range(2):
        A('dve', lambda e, i=i: e.memset(xrp[i][:, 0:P0], 0.0), writes=[('xrp_pad', i)])
    rec_w = {}

    wab32 = alloc("wab32", [128, 2048], F32, at=lng_off)
    A('sp', lambda e: e.dma_start(out=wab32[:], in_=bpack_d[:, BP_WA:BP_WA + 2048]), writes=['wab32'], slot='wab32')
    wa32 = wab32[:, 0:1024].rearrange("p (m n) -> p m n", m=8)
    wi32 = wab32[:, 1024:2048].rearrange("p (m n) -> p m n", m=8)

    def rec_front(m, hf):
        u = (2 * m + hf) % 2
        T = U[u]
        xp = xrp[m % 2]
        cw = cp[:, CP_CONVW + 4 * m: CP_CONVW + 4 * m + 4]
        cb = cp[:, CP_CONVB + m: CP_CONVB + m + 1]
        xrk = [('xrp_pad', m % 2)] + [('xrp', m % 2, t) for t in range(2 * hf + 2)]
        b0 = P0 + hf * HT
        xc, xcb = T['xc'], T['xcb']
        st = {}

        def proj(t2):
            def f():
                if hf == 0 and t2 == 0:
                    rec_w[m] = (load_w(w_in_d[:, XR_OFF + m * 128: XR_OFF + (m + 1) * 128]),
                                load_w(w_in_d[:, YR_OFF + m * 128: YR_OFF + (m + 1) * 128]))
                (wxr, wxrk), _ = rec_w[m]
                tt = 2 * hf + t2
                pb = next_ps(list(range(8)))

                def mm(e):
                    for kc in range(8):
                        ins = e.matmul(PS[pb][:], wxr[:, kc, :], xT[:, kc, tt * 512:(tt + 1) * 512],
                                       start=(kc == 0), stop=(kc == 7))
                    return ins
                A('pe', mm, reads=[wxrk] + xT_keys(tt), writes=[psk(pb)])
                A('act', lambda e: e.copy(out=xp[:, P0 + tt * 512:P0 + (tt + 1) * 512], in_=PS[pb][:]),
                  reads=[psk(pb)], writes=[('xrp', m % 2, tt)])
            return f
        st['proj0'], st['proj1'] = proj(0), proj(1)
        st['conv0'] = lambda: A('dve', lambda e: e.tensor_scalar(out=xc[:], in0=xp[:, b0:b0 + HT], scalar1=cw[:, 3:4],
                                                                 scalar2=cb, op0=ALU.mult, op1=ALU.add),
                                reads=xrk + ['cp'], writes=[('xc', u)])
        for j in range(3):
            st['conv%d' % (j + 1)] = lambda j=j: A('dve', lambda e: e.scalar_tensor_tensor(
                out=xc[:], in0=xp[:, b0 - 3 + j:b0 - 3 + j + HT], scalar=cw[:, j:j + 1], in1=xc[:],
                op0=ALU.mult, op1=ALU.add), reads=xrk + ['cp', ('xc', u)], writes=[('xc', u)])
        st['xcb'] = lambda: A('act', lambda e: e.copy(out=xcb[:], in_=xc[:]), reads=[('xc', u)], writes=[('xcb', u)])

        def gate(wbd, dst, bcol, bkey, kname, t2):
            def f():
                pb = next_ps(list(range(8)))
                A('pe', lambda e: e.matmul(PS[pb][:], wbd[:, m, :], xc[:, t2 * 512:(t2 + 1) * 512],
                                           start=True, stop=True),
                  reads=[('xc', u), 'wab32'], writes=[psk(pb)])
                A('act', lambda e: e.activation(out=dst[:, t2 * 512:(t2 + 1) * 512], in_=PS[pb][:], func=AF.Tanh,
                                                bias=sc[:, bcol + m:bcol + m + 1], scale=0.5),
                  reads=[psk(pb), bkey], writes=[(kname, u, t2)])
            return f
        for (wbd, dst, bcol, bkey, kname) in ((wa32, T['rr'], SC_HBA, 'sc_hba', 'rr'),
                                              (wi32, T['ii'], SC_HBI, 'sc_hbi', 'ii')):
            for t2 in range(2):
                st['g_%s%d' % (kname, t2)] = gate(wbd, dst, bcol, bkey, kname, t2)
        return st

    def rec_back(m, hf):
        u = (2 * m + hf) % 2
        T = U[u]
        rr, ii, a2, tt_, xc = T['rr'], T['ii'], T['a2'], T['tt'], T['xc']
        rk_all = [('rr', u, t) for t in range(2)]
        ik_all = [('ii', u, t) for t in range(2)]
        tk_all = [('tt', u, t) for t in range(2)]
        st = {}

        def expo(out, func, col):
            ap_ = sc[:, col + m:col + m + 1]
            return lambda: A('act', lambda e: e.activation(out=out[:], in_=rr[:], func=func, scale=ap_, bias=ap_),
                             reads=rk_all + CKEYS, writes=[('a2', u)] if out is a2 else (tk_all if out is tt_ else rk_all))
        st['a2'] = expo(a2, AF.Exp, SC_C16)
        st['T'] = expo(tt_, AF.Tanh, SC_C8)
        st['a'] = expo(rr, AF.Exp, SC_CM8)
        st['om'] = lambda: A('dve', lambda e: e.scalar_tensor_tensor(out=a2[:], in0=a2[:], scalar=1.0, in1=tt_[:],
                                                                     op0=ALU.add, op1=ALU.mult),
                             reads=[('a2', u)] + tk_all, writes=[('a2', u)])
        st['sqrt'] = lambda: A('act', lambda e: e.activation(out=a2[:], in_=a2[:], func=AF.Sqrt),
                               reads=[('a2', u)], writes=[('a2', u)])
        st['u1'] = lambda: A('dve', lambda e: e.scalar_tensor_tensor(out=a2[:], in0=ii[:], scalar=1.0, in1=a2[:],
                                                                     op0=ALU.add, op1=ALU.mult),
                             reads=[('a2', u)] + ik_all, writes=[('a2', u)])
        st['u2'] = lambda: A('dve', lambda e: e.scalar_tensor_tensor(out=a2[:], in0=a2[:], scalar=0.5, in1=xc[:],
                                                                     op0=ALU.mult, op1=ALU.mult),
                             reads=[('a2', u), ('xc', u)], writes=[('a2', u)])

        def scan():
            init = 0.0 if hf == 0 else hcar[:, m:m + 1]
            A('dve', lambda e: e.tensor_tensor_scan(out=ii[:], data0=rr[:], data1=a2[:], initial=init, op0=ALU.mult,
                                                    op1=ALU.add), reads=rk_all + [('a2', u)] + ik_all + ['hcar'],
              writes=ik_all)
            if hf == 0:
                A('dve', lambda e: e.tensor_copy(out=hcar[:, m:m + 1], in_=ii[:, HT - 1:HT]), reads=ik_all,
                  writes=['hcar'])
        st['scan'] = scan

        def yr(t2):
            def f():
                _, (wyr, wyrk) = rec_w[m]
                tt = 2 * hf + t2
                pb = next_ps(list(range(8)))

                def mm(e):
                    for kc in range(8):
                        ins = e.matmul(PS[pb][:], wyr[:, kc, :], xT[:, kc, tt * 512:(tt + 1) * 512],
                                       start=(kc == 0), stop=(kc == 7))
                    return ins
                A('pe', mm, reads=[wyrk] + xT_keys(tt), writes=[psk(pb)])
                A('act', lambda e: e.activation(out=tt_[:, t2 * 512:(t2 + 1) * 512], in_=PS[pb][:],
                                                func=AF.Gelu_apprx_tanh),
                  reads=[psk(pb)], writes=[('tt', u, t2)])
            return f
        st['yr0'], st['yr1'] = yr(0), yr(1)
        st['mult'] = lambda: A('dve', lambda e: e.tensor_tensor(out=recT[:, m, hf * HT:(hf + 1) * HT], in0=ii[:],
                                                                in1=tt_[:], op=ALU.mult),
                               reads=ik_all + tk_all, writes=[('recT', m, hf)])
        return st

    ORDER = [('F', 'proj0'), ('F', 'proj1'), ('B', 'a2'), ('F', 'conv0'), ('B', 'T'), ('F', 'conv1'), ('B', 'a'),
             ('B', 'om'), ('F', 'conv2'), ('B', 'sqrt'), ('F', 'conv3'), ('B', 'u1'), ('B', 'u2'),
             ('B', 'yr0'), ('F', 'g_rr0'), ('B', 'yr1'), ('F', 'g_rr1'), ('F', 'g_ii0'), ('F', 'g_ii1'),
             ('B', 'scan'), ('B', 'mult')]
    units = [(m, hf) for m in range(8) for hf in range(2)]
    prev_back = None
    for (m, hf) in units + [(None, None)]:
        fr = rec_front(m, hf) if m is not None else None
        for (w_, nm) in ORDER:
            d_ = fr if w_ == 'F' else prev_back
            if d_ is not None:
                d_[nm]()
        prev_back = rec_back(m, hf) if m is not None else None

    if dbg is not None and dbg[0] == 'recT':
        A('sp', lambda e: e.dma_start(out=dbg_d, in_=recT[:].rearrange("p c n -> p (c n)")),
          reads=[('recT', m, hf) for m in range(8) for hf in range(2)], writes=['dbg'], slot='dbg')
    if stop_after == 'recT':
        return finish([])

    walloc_reset()
    mergedT = walloc("mergedT", [128, 8, SEQ], BF16)
    gg = [walloc("gg%d" % i, [128, 512], F32) for i in range(4)]
    mm0 = [walloc("mm0_%d" % i, [128, 512], F32) for i in range(2)]
    merge_keys = [('mergedT', j, t) for j in range(8) for t in range(NTT)] + [('gg', i) for i in range(4)] + \
        [('mm0', i) for i in range(2)]
    S.alias(rec_keys, merge_keys)
    it = 0
    for j in range(8):
        wg0, wg0k = load_w(w_in_d[:, G_OFF + j * 128: G_OFF + (j + 1) * 128])
        wg1, wg1k = load_w(w_in_d[:, G_OFF + D + j * 128: G_OFF + D + (j + 1) * 128])
        wba, wbak = load_w(w_bra_d[:, j * 128:(j + 1) * 128])
        wbr, wbrk = load_w(w_brr_d[:, j * 128:(j + 1) * 128])
        for tt in range(NTT):
            pbs = [next_ps(list(range(8))) for _ in range(4)]
            attn_k = [('attnT', h, tt) for h in range(NH)]
            rec_k = [('recT', m, tt // 2) for m in range(8)]
            for (pb, wt, wk, src, sk) in ((pbs[0], wg0, wg0k, xT, xT_keys(tt)), (pbs[1], wg1, wg1k, xT, xT_keys(tt)),
                                          (pbs[2], wba, wbak, attnT, attn_k), (pbs[3], wbr, wbrk, recT, rec_k)):
                def mm(e, pb=pb, wt=wt, src=src, tt=tt):
                    for kc in range(8):
                        ins = e.matmul(PS[pb][:], wt[:, kc, :], src[:, kc, tt * 512:(tt + 1) * 512],
                                       start=(kc == 0), stop=(kc == 7))
                    return ins
                A('pe', mm, reads=[wk] + sk, writes=[psk(pb)])
            g0 = gg[(it % 2) * 2]
            g1 = gg[(it % 2) * 2 + 1]
            g0k = ('gg', (it % 2) * 2)
            g1k = ('gg', (it % 2) * 2 + 1)
            m0 = mm0[it % 2]
            m0k = ('mm0', it % 2)
            it += 1
            A('act', lambda e, g0=g0, pb=pbs[0], j=j: e.activation(out=g0[:], in_=PS[pb][:], func=AF.Sigmoid,
                                                                   bias=cp[:, CP_BGATE + j:CP_BGATE + j + 1]),
              reads=[psk(pbs[0]), 'cp'], writes=[g0k])
            A('act', lambda e, g1=g1, pb=pbs[1], j=j: e.activation(out=g1[:], in_=PS[pb][:], func=AF.Sigmoid,
                                                                   bias=cp[:, CP_BGATE + 8 + j:CP_BGATE + 8 + j + 1]),
              reads=[psk(pbs[1]), 'cp'], writes=[g1k])
            A('dve', lambda e, m0=m0, g0=g0, pb=pbs[2]: e.tensor_tensor(out=m0[:], in0=g0[:], in1=PS[pb][:], op=ALU.mult),
              reads=[g0k, psk(pbs[2])], writes=[m0k])
            A('dve', lambda e, g1=g1, pb=pbs[3]: e.tensor_tensor(out=g1[:], in0=g1[:], in1=PS[pb][:], op=ALU.mult),
              reads=[g1k, psk(pbs[3])], writes=[g1k])
            A('dve', lambda e, m0=m0, g1=g1, j=j, tt=tt: e.tensor_tensor(
                out=mergedT[:, j, tt * 512:(tt + 1) * 512], in0=m0[:], in1=g1[:], op=ALU.add),
              reads=[m0k, g1k], writes=[('mergedT', j, tt)])

    if dbg is not None and dbg[0] == 'mergedT':
        A('sp', lambda e: e.dma_start(out=dbg_d, in_=mergedT[:].rearrange("p c n -> p (c n)")),
          reads=[('mergedT', j, t) for j in range(8) for t in range(NTT)], writes=['dbg'], slot='dbg')
    if stop_after == 'mergedT':
        return finish([])

    off[0] = work0 + 8 * SEQ * 2 + 6 * 2048
    woutb = walloc("woutb", [128, 8, D], BF16)
    xin = [walloc("xin%d" % i, [128, D], F32) for i in range(2)]
    ybs = [walloc("yb%d" % i, [128, D], F32) for i in range(2)]
    x1bs = [walloc("x1b%d" % i, [128, D], BF16) for i in range(2)]
    stts = [walloc("stt%d" % i, [128, 2, 6], F32) for i in range(2)]
    mvs = [walloc("mv%d" % i, [128, 8], F32) for i in range(2)]

    def ln_keys(kp):
        return [('yb', kp, 0), ('yb', kp, 1), ('stt', kp, 0), ('stt', kp, 1), ('mv', kp), ('mv2', kp), ('mv3', kp)]
    p4_keys = [('woutb', 0), ('woutb', 1), ('xin', 0), ('xin', 1), ('x1b', 0), ('x1b', 1)] + ln_keys(0) + ln_keys(1)
    big_old = [('xT', t) for t in range(NTB)] + [('attnT', h, t) for h in range(NH) for t in range(NTT)] + \
        [('recT', m, hf) for m in range(8) for hf in range(2)]
    big_new = [('x1', t) for t in range(NTB)] + [('x1T', t) for t in range(NTB)]
    S.alias(big_old, big_new)
    S.alias(rec_keys + attn_work_keys, p4_keys)
    for n in range(2):
        A('pool', lambda e, n=n: e.dma_start(
            out=woutb[:, :, n * 512:(n + 1) * 512],
            in_=w_out_d[:, n * 512:(n + 1) * 512].rearrange("(c p) n -> p c n", p=128)),
          writes=[('woutb', n)], slot=('woutb', n))
    S.alias(['wab32'], ['lng', 'lnb'])
    A('sp', lambda e: e.dma_start(out=lng[:], in_=lnp_d[0]), writes=['lng'], slot='lng')
    A('sp', lambda e: e.dma_start(out=lnb[:], in_=lnp_d[1]), writes=['lnb'], slot='lnb')
    wdn = alloc("wdn", [128, NF, D], BF16, at=work0)
    fs = [0, 6, 11, 16, 22]

    def load_wdn(i):
        A('pool', lambda e: e.dma_start(
            out=wdn[:, fs[i]:fs[i + 1], :],
            in_=w_down_d[fs[i] * 128:fs[i + 1] * 128, :].rearrange("(c p) n -> p c n", p=128)),
          writes=[('wdn', i)], slot=('wdn', i))
    S.alias(merge_keys[32:], [('wdn', 3)])
    load_wdn(3)

    def layer_norm_block(kp, dst_ap, dst_keys, yb, stt, mv):
        ykeys = [('yb', kp, 0), ('yb', kp, 1)]
        for n in range(2):
            A('dve', lambda e, n=n: e.bn_stats(out=stt[:, n, :], in_=yb[:, n * 512:(n + 1) * 512]),
              reads=[('yb', kp, n)], writes=[('stt', kp, n)])
        A('dve', lambda e: e.bn_aggr(out=mv[:, 0:2], in_=stt[:].rearrange("p a b -> p (a b)")),
          reads=[('stt', kp, 0), ('stt', kp, 1)], writes=[('mv', kp)])
        A('act', lambda e: e.activation(out=mv[:, 2:3], in_=mv[:, 1:2], func=AF.Ln, bias=LN_EPS),
          reads=[('mv', kp)], writes=[('mv2', kp)])
        A('act', lambda e: e.activation(out=mv[:, 2:3], in_=mv[:, 2:3], func=AF.Exp, scale=-0.5),
          reads=[('mv2', kp)], writes=[('mv2', kp)])
        A('dve', lambda e: e.scalar_tensor_tensor(out=yb[:], in0=yb[:], scalar=mv[:, 0:1], in1=lng[:],
                                                  op0=ALU.subtract, op1=ALU.mult),
          reads=ykeys + [('mv', kp), 'lng'], writes=ykeys)
        A('dve', lambda e: e.scalar_tensor_tensor(out=dst_ap, in0=yb[:], scalar=mv[:, 2:3], in1=lnb[:],
                                                  op0=ALU.mult, op1=ALU.add),
          reads=ykeys + [('mv2', kp), 'lnb'], writes=dst_keys)

    tail4 = None
    for t in range(NTB):
        xi = xin[t % 2]
        xik = ('xin', t % 2)
        A('sp', lambda e, xi=xi, t=t: e.dma_start(out=xi[:], in_=x_d[t * 128:(t + 1) * 128, :]),
          writes=[xik], slot=xik)
        for n in range(2):
            pb = next_ps(list(range(6)))

            def mm(e, t=t, n=n, pb=pb):
                for kc in range(8):
                    ins = e.matmul(PS[pb][:], mergedT[:, kc, t * 128:(t + 1) * 128],
                                   woutb[:, kc, n * 512:(n + 1) * 512], start=(kc == 0), stop=(kc == 7))
                return ins
            A('pe', mm, reads=[('mergedT', j, t // 4) for j in range(8)] + [('woutb', n)], writes=[psk(pb)])
            A('dve', lambda e, xi=xi, n=n, pb=pb, yb=ybs[t % 2]: e.scalar_tensor_tensor(
                out=yb[:, n * 512:(n + 1) * 512], in0=xi[:, n * 512:(n + 1) * 512], scalar=ALPHA, in1=PS[pb][:],
                op0=ALU.mult, op1=ALU.add), reads=[xik, psk(pb)], writes=[('yb', t % 2, n)])
        if tail4 is not None:
            tail4()

        def tail4(t=t):
            x1b = x1bs[t % 2]
            x1bk = ('x1b', t % 2)
            pb = 6 + (t % 2)
            pv = PS[pb][:].bitcast(BF16).rearrange("p (c n) -> p c n", c=8)

            def tr(e):
                for c in range(8):
                    ins = e.transpose(pv[:, c, :], x1b[:, c * 128:(c + 1) * 128], ident)
                return ins
            A('pe', tr, reads=[x1bk, 'bp'], writes=[psk(pb)])
            A('act', lambda e: e.copy(out=x1T[:, :, t * 128:(t + 1) * 128], in_=pv),
              reads=[psk(pb)], writes=[('x1T', t)])
        layer_norm_block(t % 2, x1[:, t, :], [('x1', t)], ybs[t % 2], stts[t % 2], mvs[t % 2])
        A('act', lambda e, t=t, x1b=x1bs[t % 2]: e.copy(out=x1b[:], in_=x1[:, t, :]), reads=[('x1', t)],
          writes=[('x1b', t % 2)])
    tail4()

    if dbg is not None and dbg[0] == 'x1':
        A('sp', lambda e: e.dma_start(out=dbg_d, in_=x1[:].rearrange("p c n -> p (c n)")),
          reads=[('x1', t) for t in range(NTB)], writes=['dbg'], slot='dbg')
    if stop_after == 'x1':
        return finish([])

    off[0] = work0 + NF * D * 2
    hidT = walloc("hidT", [128, NF, 512], BF16)
    sg = [walloc("sg%d" % i, [128, 512], F32) for i in range(2)]
    yb5 = [walloc("yb5_%d" % i, [128, D], F32) for i in range(2)]
    stt5 = [walloc("stt5_%d" % i, [128, 2, 6], F32) for i in range(2)]
    mv5 = [walloc("mv5_%d" % i, [128, 8], F32) for i in range(2)]
    p5_keys = [('wdn', i) for i in range(3)] + [('hidT', f) for f in range(NF)] + [('sg', 0), ('sg', 1)] + \
        ln_keys(2) + ln_keys(3)
    S.alias(merge_keys + p4_keys, p5_keys)
    A('sp', lambda e: e.dma_start(out=lng[:], in_=lnp_d[2]), writes=['lng'], slot='lng')
    A('sp', lambda e: e.dma_start(out=lnb[:], in_=lnp_d[3]), writes=['lnb'], slot='lnb')
    out_keys = []
    for tq in range(NTT):
        x1T_k = [('x1T', 4 * tq + i) for i in range(4)]
        for f in range(NF):
            wg, wgk = load_w(w_gate_d[:, f * 128:(f + 1) * 128])
            wu, wuk = load_w(w_up_d[:, f * 128:(f + 1) * 128])
            if tq == 0 and f % 4 == 3 and f // 4 < 3:
                load_wdn(f // 4)
            pg = next_ps(list(range(8)))
            pu = next_ps(list(range(8)))
            for (pb, wt, wk) in ((pg, wg, wgk), (pu, wu, wuk)):
                def mm(e, pb=pb, wt=wt, tq=tq):
                    for kc in range(8):
                        ins = e.matmul(PS[pb][:], wt[:, kc, :], x1T[:, kc, tq * 512:(tq + 1) * 512],
                                       start=(kc == 0), stop=(kc == 7))
                    return ins
                A('pe', mm, reads=[wk] + x1T_k, writes=[psk(pb)])
            s_ = sg[f % 2]
            sk = ('sg', f % 2)
            A('act', lambda e, s_=s_, pg=pg: e.activation(out=s_[:], in_=PS[pg][:], func=AF.Silu),
              reads=[psk(pg)], writes=[sk])
            A('dve', lambda e, s_=s_, pu=pu, f=f: e.tensor_tensor(out=hidT[:, f, :], in0=s_[:], in1=PS[pu][:],
                                                                  op=ALU.mult),
              reads=[sk, psk(pu)], writes=[('hidT', f)])
        for tb in range(4):
            t = 4 * tq + tb
            for n in range(2):
                pb = next_ps(list(range(8)))

                def mm(e, tb=tb, n=n, pb=pb):
                    for f in range(NF):
                        ins = e.matmul(PS[pb][:], hidT[:, f, tb * 128:(tb + 1) * 128],
                                       wdn[:, f, n * 512:(n + 1) * 512], start=(f == 0), stop=(f == NF - 1))
                    return ins
                A('pe', mm, reads=[('hidT', f) for f in range(NF)] + [('wdn', i) for i in range(4)],
                  writes=[psk(pb)])
                A('dve', lambda e, t=t, n=n, pb=pb, yb=yb5[t % 2]: e.scalar_tensor_tensor(
                    out=yb[:, n * 512:(n + 1) * 512], in0=x1[:, t, n * 512:(n + 1) * 512], scalar=ALPHA,
                    in1=PS[pb][:], op0=ALU.mult, op1=ALU.add), reads=[('x1', t), psk(pb)],
                  writes=[('yb', 2 + t % 2, n)])
            kp = 2 + t % 2
            layer_norm_block(kp, yb5[t % 2][:], [('yb', kp, 0), ('yb', kp, 1)], yb5[t % 2], stt5[t % 2], mv5[t % 2])
            ok = ('out', t)
            A('sp', lambda e, t=t, yb=yb5[t % 2]: e.dma_start(out=out_d[t * 128:(t + 1) * 128, :], in_=yb[:]),
              reads=[('yb', kp, 0), ('yb', kp, 1)], writes=[ok], slot=('ost', t % 2))
            out_keys.append(ok)

    return finish(out_keys)


def _pack_consts(inp):
    f32 = np.float32
    cp = np.zeros((128, CP_W), f32)
    p = np.arange(128, dtype=np.float64)
    for h in range(NH):
        slope = 2.0 ** (-8.0 * (h + 1) / NH)
        for j in range(-3, 17):
            cp[:, CP_ALIBI + h * 20 + j + 3] = (slope * (p - 128.0 * j)).astype(f32)
    cp[:, CP_BGATE:CP_BGATE + 16] = inp["b_gate"][0].reshape(16, 128).T
    cp[:, CP_CONVW:CP_CONVW + 32] = inp["conv_w"][0].reshape(4, 8, 128).transpose(2, 1, 0).reshape(128, 32)
    cp[:, CP_CONVB:CP_CONVB + 8] = inp["conv_b"][0].reshape(8, 128).T
    cp[:, CP_BA:CP_BA + 8] = inp["b_a"][0].reshape(8, 128).T
    cp[:, CP_BI:CP_BI + 8] = inp["b_i"][0].reshape(8, 128).T
    cp[:, CP_LRU:CP_LRU + 8] = inp["lru_lambda"][0].reshape(8, 128).T
    cp[:, CP_GSUB:CP_GSUB + 128] = np.broadcast_to(inp["subln_g"][0], (128, 128))
    for i, k in enumerate(("lambda_q1", "lambda_k1", "lambda_q2", "lambda_k2")):
        cp[:, CP_LAMV + 64 * i:CP_LAMV + 64 * (i + 1)] = np.broadcast_to(inp[k][0], (128, 64))
    bp = np.zeros((128, BP_W), f32)
    bp[:, BP_ID:BP_ID + 128] = np.eye(128, dtype=f32)
    bp[:, BP_TRI:BP_TRI + 128] = np.triu(np.ones((128, 128), f32))
    bp[:, BP_MASKB:BP_MASKB + 128] = np.tril(np.full((128, 128), -30000.0, f32), -1)
    for name, base in (("w_a", BP_WA), ("w_i", BP_WI)):
        w = inp[name][0]
        for m in range(8):
            for b in range(2):
                bp[b * 64:(b + 1) * 64, base + m * 128 + b * 64: base + m * 128 + (b + 1) * 64] = w[2 * m + b]
    lnp = np.stack([np.broadcast_to(inp[k][0], (128, D)) for k in ("ln1_g", "ln1_b", "ln2_g", "ln2_b")]).astype(f32)
    return cp, bp, np.ascontiguousarray(lnp)


_NC_CACHE = {}


def _get_nc(dbg=None):
    key = dbg
    if key not in _NC_CACHE:
        _NC_CACHE[key] = build_nc(dbg)
    return _NC_CACHE[key]


def kernel(**inputs):
    inp = {k: np.asarray(v) for k, v in inputs.items()}
    cp, bp, lnp = _pack_consts(inp)
    shared = {
        "w_in": np.ascontiguousarray(inp["w_in"][0]),
        "w_br_attn": np.ascontiguousarray(inp["w_br_attn"][0]),
        "w_br_rec": np.ascontiguousarray(inp["w_br_rec"][0]),
        "w_out": np.ascontiguousarray(inp["w_out"][0]),
        "w_gate": np.ascontiguousarray(inp["w_gate"][0]),
        "w_up": np.ascontiguousarray(inp["w_up"][0]),
        "w_down": np.ascontiguousarray(inp["w_down"][0]),
        "cpack": cp, "bpack": bp, "lnp": lnp,
    }
    nc = _get_nc()
    in_maps = []
    for c in range(8):
        m = dict(shared)
        m["x"] = np.ascontiguousarray(inp["x"][c])
        in_maps.append(m)
    res = run_bass_kernel_spmd(nc, in_maps, core_ids=list(range(8)))
    out = np.stack([np.asarray(r["out"]) for r in res.results], axis=0)
    return out.astype(np.float32)
```

```python
import math
from contextlib import ExitStack

import numpy as np
import concourse.bass as bass
import concourse.mybir as mybir
from concourse.bass_utils import run_bass_kernel_spmd

F32 = mybir.dt.float32
BF16 = mybir.dt.bfloat16
AF = mybir.ActivationFunctionType
ALU = mybir.AluOpType
AX = mybir.AxisListType

ENGS = ['pe', 'act', 'dve', 'pool', 'sp']
ENGATTR = {'pe': 'tensor', 'act': 'scalar', 'dve': 'vector', 'pool': 'gpsimd', 'sp': 'sync'}

SAME_ENGINE_SYNC = True

D = 1024
SEQ = 2048
NH = 8
HD = 64
DFF = 2816
NF = DFF // 128
Q_OFF, K_OFF, V_OFF, XR_OFF, YR_OFF, G_OFF = 0, 1024, 2048, 3072, 4096, 5120
IN_W = 7168
LN_EPS = 1e-5
ALPHA = 2.0 ** 0.25
LAM_INIT = 0.8 - 0.6 * math.exp(0.0)
SCALE = HD ** -0.5
NTB = SEQ // 128
NTT = SEQ // 512


class Op:
    __slots__ = ('eng', 'fn', 'deps', 'slot', 'sem', 'val', 'needs_inc', 'waits', 'seq')


class Sched:
    def __init__(self, same_engine_sync=True):
        self.ops = {e: [] for e in ENGS}
        self.last_w = {}
        self.readers = {}
        self.same = same_engine_sync
        self.seq = 0

    def add(self, eng, fn, reads=(), writes=(), slot=None):
        op = Op()
        op.eng = eng
        op.fn = fn
        op.slot = slot
        op.needs_inc = slot is not None
        self.seq += 1
        op.seq = self.seq
        deps = {}
        for k in reads:
            w = self.last_w.get(k)
            if w is not None:
                deps[id(w)] = w
        for k in writes:
            w = self.last_w.get(k)
            if w is not None:
                deps[id(w)] = w
            for r in self.readers.get(k, ()):
                deps[id(r)] = r
        op.deps = list(deps.values())
        for k in reads:
            lst = self.readers.setdefault(k, [])
            if op.slot is None:
                lst[:] = [r for r in lst if not (r.slot is None and r.eng == eng)]
            lst.append(op)
        for k in writes:
            self.last_w[k] = op
            self.readers[k] = []
        self.ops[eng].append(op)
        return op

    def alias(self, old_keys, new_keys):
        acc = {}
        for k in old_keys:
            w = self.last_w.get(k)
            if w is not None:
                acc[id(w)] = w
            for r in self.readers.get(k, ()):
                acc[id(r)] = r
        best = {}
        keep = []
        for o in acc.values():
            if o.slot is not None:
                keep.append(o)
            else:
                b = best.get(o.eng)
                if b is None or b.seq < o.seq:
                    best[o.eng] = o
        keep.extend(best.values())
        for k in new_keys:
            self.last_w.pop(k, None)
            self.readers[k] = list(keep)

    def _sync(self, op, d):
        if d.slot is not None:
            return True
        if d.eng != op.eng:
            return True
        if op.slot is not None:
            return True
        if d.eng == 'pe':
            return False
        return self.same

    def finalize(self, nc, stack):
        for e in ENGS:
            for op in self.ops[e]:
                for d in op.deps:
                    if d.slot is None and self._sync(op, d):
                        d.needs_inc = True
        self.esem = {e: stack.enter_context(nc.semaphore('s_' + e)) for e in ENGS}
        slotsem = {}
        slotcnt = {}
        allops = sorted((op for e in ENGS for op in self.ops[e]), key=lambda o: o.seq)
        for op in allops:
            if op.slot is not None:
                if op.slot not in slotsem:
                    slotsem[op.slot] = stack.enter_context(nc.semaphore('d_%d' % len(slotsem)))
                    slotcnt[op.slot] = 0
                slotcnt[op.slot] += 1
                op.sem = slotsem[op.slot]
                op.val = 16 * slotcnt[op.slot]
        for e in ENGS:
            tick = 0
            for op in self.ops[e]:
                if op.slot is None and op.needs_inc:
                    tick += 1
                    op.sem = self.esem[e]
                    op.val = tick
        self.nsem = len(slotsem) + len(ENGS)
        for e in ENGS:
            waited = {}
            for op in self.ops[e]:
                w = {}
                for d in op.deps:
                    if not self._sync(op, d):
                        continue
                    key = id(d.sem)
                    if key not in w or w[key][1] < d.val:
                        w[key] = (d.sem, d.val)
                op.waits = []
                for key, (sem, val) in w.items():
                    if waited.get(key, 0) >= val:
                        continue
                    waited[key] = val
                    op.waits.append((sem, val))

    def emit(self, nc):
        with nc.Block() as block:
            for e in ENGS:
                ops = self.ops[e]
                if not ops:
                    continue

                def body(eng, ops=ops):
                    for op in ops:
                        for (sem, val) in op.waits:
                            eng.wait_ge(sem, val)
                        if op.fn is None:
                            continue
                        ins = op.fn(eng)
                        if op.slot is not None:
                            ins.then_inc(op.sem, 16)
                        elif op.needs_inc:
                            ins.then_inc(op.sem, 1)

                getattr(block, ENGATTR[e])(body)


CP_ALIBI = 0
CP_BGATE = 160
CP_CONVW = 176
CP_CONVB = 208
CP_BA = 216
CP_BI = 224
CP_LRU = 232
CP_GSUB = 240
CP_LAMV = 368
CP_W = 624
BP_ID = 0
BP_TRI = 128
BP_WA = 256
BP_WI = 1280
BP_MASKB = 2304
BP_W = 2432


def build_nc(dbg=None, stop_after=None):
    nc = bass.Bass("TRN2", target_bir_lowering=False)

    def din(name, shape):
        return nc.dram_tensor(name, shape, F32, kind="ExternalInput").ap()

    x_d = din("x", [SEQ, D])
    w_in_d = din("w_in", [D, IN_W])
    w_bra_d = din("w_br_attn", [D, D])
    w_brr_d = din("w_br_rec", [D, D])
    w_out_d = din("w_out", [D, D])
    w_gate_d = din("w_gate", [D, DFF])
    w_up_d = din("w_up", [D, DFF])
    w_down_d = din("w_down", [DFF, D])
    cpack_d = din("cpack", [128, CP_W])
    bpack_d = din("bpack", [128, BP_W])
    lnp_d = din("lnp", [4, 128, D])
    out_d = nc.dram_tensor("out", [SEQ, D], F32, kind="ExternalOutput").ap()
    dbg_d = None
    if dbg is not None:
        dbg_d = nc.dram_tensor("dbg", [128, dbg[1]], dbg[2], kind="ExternalOutput").ap()

    S = Sched(same_engine_sync=SAME_ENGINE_SYNC)
    A = S.add

    def finish(out_keys):
        fin_reads = list(out_keys)
        if dbg is not None:
            fin_reads.append('dbg')
        A('sp', None, reads=fin_reads)
        stack = ExitStack()
        S.finalize(nc, stack)
        S.emit(nc)
        stack.close()
        nc._sched_stats = {e: len(S.ops[e]) for e in ENGS}
        return nc

    SB_BASE, SB_TOP = 16512, 229344
    off = [SB_BASE]

    def alloc(name, shape, dt, at=None):
        nbytes = int(np.prod(shape[1:])) * (4 if dt == F32 else 2)
        nbytes = (nbytes + 31) // 32 * 32
        if at is None:
            o = off[0]
            off[0] += nbytes
        else:
            o = at
        t = nc.alloc_sbuf_tensor_at(name, list(shape), dt, offset=o)
        return t

    cp = alloc("cp", [128, CP_W], F32)
    bp = alloc("bp", [128, BP_W], BF16)
    sc = alloc("sc", [128, 96], F32)
    lng = alloc("lng", [128, D], F32)
    lnb = alloc("lnb", [128, D], F32)
    big0 = off[0]
    xT = alloc("xT", [128, 8, SEQ], BF16)
    attnT = alloc("attnT", [128, 8, SEQ], BF16)
    recT = alloc("recT", [128, 8, SEQ], BF16)
    x1 = alloc("x1", [128, NTB, D], F32, at=big0)
    x1T = alloc("x1T", [128, 8, SEQ], BF16, at=big0 + 65536)
    ring0 = off[0]
    NRING = 8
    ring = [alloc("ring%d" % i, [128, 8, 128], BF16) for i in range(NRING)]
    ringbig = [alloc("ringbig%d" % i, [128, 8, 512], BF16, at=ring0 + i * 8192) for i in range(2)]
    work0 = off[0]
    WORK = SB_TOP - work0
    assert WORK >= 80 * 1024 - 1024, WORK

    def walloc_reset():
        off[0] = work0

    def walloc(name, shape, dt):
        t = alloc(name, shape, dt)
        assert off[0] <= SB_TOP, (name, off[0])
        return t

    psS = [nc.alloc_psum_tensor("psS%d" % i, [128, 1024], F32) for i in range(2)]
    PS = [psS[0][:, 0:512], psS[0][:, 512:1024], psS[1][:, 0:512], psS[1][:, 512:1024]] + \
        [nc.alloc_psum_tensor("ps%d" % i, [128, 512], F32) for i in range(4, 8)]

    def psk(i):
        return ('ps', i)

    ident = bp[:, BP_ID:BP_ID + 128]
    tri = bp[:, BP_TRI:BP_TRI + 128]
    maskb = bp[:, BP_MASKB:BP_MASKB + 128]

    A('sp', lambda e: e.dma_start(out=cp[:], in_=cpack_d), writes=['cp'], slot='cp')
    A('pool', lambda e: e.dma_start(out=bp[:], in_=bpack_d), writes=['bp'], slot='bp')

    SC_NEGLAM, SC_S1, SC_S2, SC_E1, SC_E2 = 0, 1, 2, 3, 4
    SC_CM8, SC_C16, SC_C8 = 8, 16, 24
    SC_TMP = 32

    walloc_reset()
    lamtmp = walloc("lamtmp", [128, 2, 64], F32)
    lamv = cp[:, CP_LAMV:CP_LAMV + 256].rearrange("p (a n) -> p a n", a=4)
    A('dve', lambda e: e.tensor_tensor(out=lamtmp[:, 0, :], in0=lamv[:, 0, :], in1=lamv[:, 1, :], op=ALU.mult),
      reads=['cp'], writes=['lamtmp0'])
    A('dve', lambda e: e.tensor_tensor(out=lamtmp[:, 1, :], in0=lamv[:, 2, :], in1=lamv[:, 3, :], op=ALU.mult),
      reads=['cp'], writes=['lamtmp1'])
    A('dve', lambda e: e.reduce_sum(out=sc[:, SC_S1:SC_S1 + 1], in_=lamtmp[:, 0, :], axis=AX.X),
      reads=['lamtmp0'], writes=['sc_s1'])
    A('dve', lambda e: e.reduce_sum(out=sc[:, SC_S2:SC_S2 + 1], in_=lamtmp[:, 1, :], axis=AX.X),
      reads=['lamtmp1'], writes=['sc_s2'])
    A('act', lambda e: e.activation(out=sc[:, SC_E1:SC_E1 + 1], in_=sc[:, SC_S1:SC_S1 + 1], func=AF.Exp),
      reads=['sc_s1'], writes=['sc_e1'])
    A('act', lambda e: e.activation(out=sc[:, SC_E2:SC_E2 + 1], in_=sc[:, SC_S2:SC_S2 + 1], func=AF.Exp),
      reads=['sc_s2'], writes=['sc_e2'])
    A('dve', lambda e: e.tensor_tensor(out=sc[:, SC_TMP:SC_TMP + 1], in0=sc[:, SC_E2:SC_E2 + 1],
                                       in1=sc[:, SC_E1:SC_E1 + 1], op=ALU.subtract),
      reads=['sc_e1', 'sc_e2'], writes=['sc_tmp'])
    A('dve', lambda e: e.tensor_scalar(out=sc[:, SC_NEGLAM:SC_NEGLAM + 1], in0=sc[:, SC_TMP:SC_TMP + 1],
                                       scalar1=-LAM_INIT, scalar2=None, op0=ALU.add),
      reads=['sc_tmp'], writes=['neglam'])
    gsub = cp[:, CP_GSUB:CP_GSUB + 128]
    A('dve', lambda e: e.tensor_scalar(out=gsub, in0=gsub, scalar1=1.0 - LAM_INIT, scalar2=None, op0=ALU.mult),
      reads=['cp'], writes=['gsub'])
    A('act', lambda e: e.activation(out=sc[:, SC_TMP + 8:SC_TMP + 16], in_=cp[:, CP_LRU:CP_LRU + 8], func=AF.Exp,
                                    scale=-1.0), reads=['cp'], writes=['sp_z'])
    A('act', lambda e: e.activation(out=sc[:, SC_TMP + 16:SC_TMP + 24], in_=sc[:, SC_TMP + 8:SC_TMP + 16],
                                    func=AF.Ln, bias=1.0), reads=['sp_z'], writes=['sp_l'])
    for (col, mul) in ((SC_CM8, -4.0), (SC_C16, -8.0), (SC_C8, 4.0)):
        A('dve', lambda e, col=col, mul=mul: e.tensor_scalar(out=sc[:, col:col + 8],
                                                             in0=sc[:, SC_TMP + 16:SC_TMP + 24],
                                                             scalar1=mul, scalar2=None, op0=ALU.mult),
          reads=['sp_l'], writes=['sc_c%d' % col])
    CKEYS = ['sc_c%d' % c for c in (SC_CM8, SC_C16, SC_C8)]
    SC_HBA, SC_HBI = 64, 72
    A('dve', lambda e: e.tensor_scalar(out=sc[:, SC_HBA:SC_HBA + 8], in0=cp[:, CP_BA:CP_BA + 8], scalar1=0.5,
                                       scalar2=None, op0=ALU.mult), reads=['cp'], writes=['sc_hba'])
    A('dve', lambda e: e.tensor_scalar(out=sc[:, SC_HBI:SC_HBI + 8], in0=cp[:, CP_BI:CP_BI + 8], scalar1=0.5,
                                       scalar2=None, op0=ALU.mult), reads=['cp'], writes=['sc_hbi'])

    ring_i = [0]

    def load_w(src2d):
        s = ring_i[0] % NRING
        ring_i[0] += 1
        t = ring[s]
        A('pool', lambda e: e.dma_start(out=t[:], in_=src2d.rearrange("(c p) n -> p c n", p=128)),
          writes=[('ring', s)], slot=('ring', s))
        return t, ('ring', s)

    ps_i = [0]

    def next_ps(banks):
        b = banks[ps_i[0] % len(banks)]
        ps_i[0] += 1
        return b

    def xT_keys(tt):
        return [('xT', 4 * tt + i) for i in range(4)]

    NXB = 4
    xb = [walloc("xb%d" % i, [128, D], BF16) for i in range(NXB)]
    def x_load(t):
        b = xb[t % NXB]
        bk = ('xb', t % NXB)
        A('pool', lambda e: e.dma_start(out=b[:], in_=x_d[t * 128:(t + 1) * 128, :]), writes=[bk], slot=bk)

    def x_block(t):
        b = xb[t % NXB]
        bk = ('xb', t % NXB)
        pb = 6 + (t % 2)
        pv = PS[pb][:].bitcast(BF16).rearrange("p (c n) -> p c n", c=8)

        def tr(e):
            for c in range(8):
                ins = e.transpose(pv[:, c, :], b[:, c * 128:(c + 1) * 128], ident)
            return ins
        A('pe', tr, reads=[bk, 'bp'], writes=[psk(pb)])
        A('dve', lambda e: e.tensor_copy(out=xT[:, :, t * 128:(t + 1) * 128], in_=pv),
          reads=[psk(pb)], writes=[('xT', t)])

    v_sb = walloc("v_sb", [128, NTB, NH, 130], BF16)
    qT = [walloc("qT%d" % i, [128, SEQ], BF16) for i in range(2)]
    kT = [walloc("kT%d" % i, [128, SEQ], BF16) for i in range(2)]
    PT = [walloc("PT%d" % i, [128, 2, 512], BF16) for i in range(2)]
    attn_epi_start = off[0]
    oraw = [walloc("oraw%d" % i, [128, 8, 130], F32) for i in range(2)]
    ojunk = walloc("ojunk", [128, 4, 128], F32)
    atok = [walloc("atok%d" % i, [128, 4, 128], BF16) for i in range(2)]
    asc = [walloc("asc%d" % i, [128, 16], F32) for i in range(2)]
    attn_core_keys = ['v_ones'] + [('v', t, n) for t in range(NTB) for n in range(2)] + \
        [('qT', i, t) for i in range(2) for t in range(NTT)] + [('kT', i, t) for i in range(2) for t in range(NTT)] + \
        [('PT', i, q) for i in range(2) for q in range(4)] + \
        [('xb', i) for i in range(4)] + ['lamtmp0', 'lamtmp1']
    attn_epi_keys = ['ojunk'] + [(nm, i) for nm in ('atok', 'rinv', 'ss') for i in range(2)] + \
        [('oraw', i, b_) for i in range(2) for b_ in range(3)]
    attn_epi_off = None
    attn_work_keys = attn_core_keys + attn_epi_keys

    A('dve', lambda e: e.memset(v_sb[:, :, :, 128:130], 1.0), writes=['v_ones'])
    def v_block(t):
        for n in range(2):
            wt = ringbig[n]
            wkeys = [('ring', 4 * n + i) for i in range(4)]
            pb = next_ps([0, 1, 2, 3, 4, 5])

            def mm(e, wt=wt, pb=pb):
                for kc in range(8):
                    ins = e.matmul(PS[pb][:], xT[:, kc, t * 128:(t + 1) * 128], wt[:, kc, :],
                                   start=(kc == 0), stop=(kc == 7))
                return ins
            A('pe', mm, reads=wkeys + [('xT', t)], writes=[psk(pb)])
            A('dve', lambda e, n=n, pb=pb: e.tensor_copy(
                out=v_sb[:, t, 4 * n:4 * n + 4, 0:128], in_=PS[pb][:].rearrange("p (h d) -> p h d", h=4)),
              reads=[psk(pb)], writes=[('v', t, n)])

    for t in range(NXB):
        x_load(t)
    for n in range(2):
        A('pool', lambda e, n=n: e.dma_start(
            out=ringbig[n][:],
            in_=w_in_d[:, V_OFF + n * 512:V_OFF + (n + 1) * 512].rearrange("(c p) n -> p c n", p=128)),
          writes=[('ring', 4 * n + i) for i in range(4)], slot=('ring', 4 * n))
    for t in range(NTB):
        x_block(t)
        if t + NXB < NTB:
            x_load(t + NXB)
        if t >= 1:
            v_block(t - 1)
    v_block(NTB - 1)
    ring_i[0] = 0

    if dbg is not None and dbg[0] == 'xT':
        A('sp', lambda e: e.dma_start(out=dbg_d, in_=xT[:].rearrange("p c n -> p (c n)")),
          reads=[('xT', t) for t in range(NTB)], writes=['dbg'], slot='dbg')
    if stop_after == 'xT':
        return finish([])

    def proj_steps(h):
        hb = h % 2
        steps = []
        for which, base, dst, kname in (('q', Q_OFF, qT[hb], 'qT'), ('k', K_OFF, kT[hb], 'kT')):
            holder = {}

            def ld(base=base, holder=holder):
                holder['w'] = load_w(w_in_d[:, base + h * 128: base + (h + 1) * 128])
            for tt in range(NTT):
                def step(tt=tt, dst=dst, kname=kname, holder=holder, ld=ld):
                    if tt == 0:
                        ld()
                    wt, wk = holder['w']
                    pb = 7

                    def mm(e):
                        for kc in range(8):
                            ins = e.matmul(PS[pb][:], wt[:, kc, :], xT[:, kc, tt * 512:(tt + 1) * 512],
                                           start=(kc == 0), stop=(kc == 7))
                        return ins
                    A('pe', mm, reads=[wk] + xT_keys(tt), writes=[psk(pb)])
                    A('act', lambda e: e.copy(out=dst[:, tt * 512:(tt + 1) * 512], in_=PS[pb][:]),
                      reads=[psk(pb)], writes=[(kname, hb, tt)])
                steps.append(step)
        return steps

    def proj_substeps(h):
        hb = h % 2
        subs = []
        for which, base, dst, kname in (('q', Q_OFF, qT[hb], 'qT'), ('k', K_OFF, kT[hb], 'kT')):
            holder = {}
            for tt in range(NTT):
                for part in range(4):
                    def sub(tt=tt, part=part, dst=dst, kname=kname, holder=holder, base=base):
                        if tt == 0 and part == 0:
                            holder['w'] = load_w(w_in_d[:, base + h * 128: base + (h + 1) * 128])
                        wt, wk = holder['w']

                        def mm(e):
                            for kc in (2 * part, 2 * part + 1):
                                ins = e.matmul(PS[7][:], wt[:, kc, :], xT[:, kc, tt * 512:(tt + 1) * 512],
                                               start=(kc == 0), stop=(kc == 7))
                            return ins
                        A('pe', mm, reads=[wk] + xT_keys(tt), writes=[psk(7)])
                        if part == 3:
                            A('act', lambda e: e.copy(out=dst[:, tt * 512:(tt + 1) * 512], in_=PS[7][:]),
                              reads=[psk(7)], writes=[(kname, hb, tt)])
                    subs.append(sub)
        return subs

    pq = []
    pq_done = [0]

    def emit_proj(n):
        for _ in range(n):
            if pq:
                pq.pop(0)()
                pq_done[0] += 1

    def finish_proj_group():
        while pq and pq_done[0] % 4 != 0:
            emit_proj(1)

    epi_i = [0]
    deferred = []

    def tick():
        for d_ in deferred:
            d_[0] -= 1
        while deferred and deferred[0][0] <= 0:
            deferred.pop(0)[1]()

    def flush_deferred():
        while deferred:
            deferred.pop(0)[1]()

    steps_left = [0]

    def attn_head(h, interleave):
        hb = h % 2
        steps_left[0] = 40
        W = 256 if h == 0 else 512
        for qt in range(NTT):
            q0 = qt * 512
            nkb = 4 * qt + 4
            par = epi_i[0] % 2
            epi_i[0] += 1
            pend = None
            for kb in range(nkb):
                sl = kb % 2
                i0 = max(0, kb - 4 * qt)
                c0 = i0 * 128
                skeys = [psk(2 * sl), psk(2 * sl + 1)]

                diag = kb >= 4 * qt

                def smm(e, sl=sl, kb=kb, c0=c0, q0=q0, diag=diag):
                    for c in range(2):
                        ins = e.matmul(psS[sl][:, c * 512 + c0:(c + 1) * 512],
                                       kT[hb][64 * c:64 * c + 64, kb * 128:(kb + 1) * 128],
                                       qT[hb][64 * c:64 * c + 64, q0 + c0:q0 + 512],
                                       start=True, stop=not diag, skip_group_check=True)
                    if diag:
                        for c in range(2):
                            ins = e.matmul(psS[sl][:, c * 512 + c0:c * 512 + c0 + 128], ident, maskb,
                                           start=False, stop=True, skip_group_check=True)
                    return ins
                A('pe', smm, reads=[('kT', hb, kb // 4), ('qT', hb, qt), 'bp'], writes=skeys)
                sv = psS[sl][:].rearrange("p (c n) -> p c n", c=2)
                cs = c0
                while cs < 512:
                    ce = min(512, (cs // W + 1) * W)
                    j = (q0 + (cs // W) * W + W // 2 - kb * 128) // 128
                    assert -3 <= j <= 16
                    bias = cp[:, CP_ALIBI + h * 20 + j + 3: CP_ALIBI + h * 20 + j + 4]
                    A('act', lambda e, sl=sl, cs=cs, ce=ce, bias=bias, sv=sv: e.activation(
                        out=PT[sl][:, :, cs:ce], in_=sv[:, :, cs:ce], func=AF.Exp, bias=bias, scale=SCALE),
                      reads=skeys + ['cp'], writes=[('PT', sl, qi_) for qi_ in range(cs // 128, ce // 128)])
                    cs = ce
                if pend is not None:
                    pend()

                def pv(kb=kb, sl=sl, i0=i0, qt=qt):
                    def f(e):
                        for c in range(2):
                            for qi in range(i0, 4):
                                qb = 4 * qt + qi
                                a = c * 4 + qi
                                acc = PS[4 + a // 3][:, (a % 3) * 130:(a % 3) * 130 + 129]
                                ins = e.matmul(acc, PT[sl][:, c, qi * 128:(qi + 1) * 128], v_sb[:, kb, h, 0:129],
                                               start=(kb == 0 and a % 3 == 0), stop=(kb == qb),
                                               skip_group_check=True)
                        return ins
                    A('pe', f, reads=[('PT', sl, qi_) for qi_ in range(i0, 4)] + [('v', kb, h // 4), 'v_ones'],
                      writes=[psk(4), psk(5), psk(6)])
                pend = pv
                tick()
                steps_left[0] -= 1
                emit_proj(-(-len(pq) // max(1, steps_left[0])) if pq else 0)
            pend()
            orw = oraw[par]
            while any(d_[2] == par for d_ in deferred):
                deferred.pop(0)[1]()
            for b_ in range(3):
                na = 3 if b_ < 2 else 2
                A('dve', lambda e, b_=b_, na=na, orw=orw: e.tensor_copy(
                    out=orw[:, 3 * b_:3 * b_ + na, :].rearrange("p a n -> p (a n)"), in_=PS[4 + b_][:, 0:130 * na]),
                  reads=[psk(4 + b_)], writes=[('oraw', par, b_)])
            okeys = [('oraw', par, b_) for b_ in range(3)]
            rinv = asc[par][:, 0:8]
            ss = asc[par][:, 8:12]
            A('dve', lambda e, orw=orw, rinv=rinv: e.reciprocal(out=rinv.unsqueeze(2), in_=orw[:, :, 128:129]),
              reads=okeys, writes=[('rinv', par)])
            A('dve', lambda e, rinv=rinv: e.tensor_scalar(out=rinv[:, 4:8], in0=rinv[:, 4:8],
                                                          scalar1=sc[:, SC_NEGLAM:SC_NEGLAM + 1], scalar2=None,
                                                          op0=ALU.mult),
              reads=[('rinv', par), 'neglam'], writes=[('rinv', par)])
            A('dve', lambda e, orw=orw, rinv=rinv: e.tensor_tensor(
                out=orw[:, :, 0:128], in0=orw[:, :, 0:128], in1=rinv.unsqueeze(2).broadcast_to([128, 8, 128]),
                op=ALU.mult), reads=okeys + [('rinv', par)], writes=okeys)
            A('dve', lambda e, orw=orw: e.tensor_tensor(out=orw[:, 0:4, 0:128], in0=orw[:, 0:4, 0:128],
                                                        in1=orw[:, 4:8, 0:128], op=ALU.add),
              reads=okeys, writes=okeys)
            A('dve', lambda e, orw=orw: e.tensor_tensor(out=ojunk[:], in0=orw[:, 0:4, 0:128], in1=orw[:, 0:4, 0:128],
                                                        op=ALU.mult), reads=okeys, writes=['ojunk'])
            A('dve', lambda e, ss=ss: e.reduce_sum(out=ss, in_=ojunk[:], axis=AX.X), reads=['ojunk'],
              writes=[('ss', par)])
            atk = atok[par]

            def stage1(ss=ss, orw=orw, atk=atk, okeys=okeys, par=par):
                A('act', lambda e: e.activation(out=ss, in_=ss, func=AF.Ln, bias=LN_EPS, scale=1.0 / 128),
                  reads=[('ss', par)], writes=[('ss', par)])
                A('act', lambda e: e.activation(out=ss, in_=ss, func=AF.Exp, scale=-0.5),
                  reads=[('ss', par)], writes=[('ss', par)])
                A('dve', lambda e: e.tensor_tensor(
                    out=orw[:, 0:4, 0:128], in0=orw[:, 0:4, 0:128], in1=ss.unsqueeze(2).broadcast_to([128, 4, 128]),
                    op=ALU.mult), reads=okeys + [('ss', par)], writes=okeys)
                A('dve', lambda e: e.tensor_tensor(
                    out=atk[:], in0=orw[:, 0:4, 0:128], in1=gsub.unsqueeze(1).broadcast_to([128, 4, 128]),
                    op=ALU.mult), reads=okeys + ['gsub'], writes=[('atok', par)])

            def stage2(atk=atk, par=par, q0=q0, h=h, qt=qt):
                pb = 7
                finish_proj_group()
                pv_ = PS[pb][:].bitcast(BF16)[:, 0:512].rearrange("p (c n) -> p c n", c=4)

                def tr(e):
                    for qi in range(4):
                        ins = e.transpose(pv_[:, qi, :], atk[:, qi, :], ident)
                    return ins
                A('pe', tr, reads=[('atok', par), 'bp'], writes=[psk(pb)])
                A('act', lambda e: e.copy(out=attnT[:, h, q0:q0 + 512], in_=pv_.rearrange("p c n -> p (c n)")),
                  reads=[psk(pb)], writes=[('attnT', h, qt)])
            deferred.append([7, stage1, par])
            deferred.append([10, stage2, par])

    st0 = proj_steps(0)
    for s_ in st0:
        s_()
    for h in range(NH):
        pq_done[0] = 0
        pq.extend(proj_substeps(h + 1) if h + 1 < NH else [])
        attn_head(h, None)
        emit_proj(len(pq))
    flush_deferred()

    if dbg is not None and dbg[0] == 'attnT':
        A('sp', lambda e: e.dma_start(out=dbg_d, in_=attnT[:].rearrange("p c n -> p (c n)")),
          reads=[('attnT', h, qt) for h in range(NH) for qt in range(NTT)], writes=['dbg'], slot='dbg')
    if stop_after == 'attnT':
        return finish([])

    walloc_reset()
    HT = SEQ // 2
    P0 = 8
    xrp = [walloc("xrp%d" % i, [128, P0 + SEQ], F32) for i in range(2)]
    U = []
    for i in range(2):
        U.append(dict(xc=walloc("xc%d" % i, [128, HT], F32), rr=walloc("rr%d" % i, [128, HT], F32),
                      ii=walloc("ii%d" % i, [128, HT], F32), a2=walloc("a2%d" % i, [128, HT], F32),
                      tt=walloc("tt%d" % i, [128, HT], F32), xcb=walloc("xcb%d" % i, [128, HT], BF16)))
    hcar = walloc("hcar", [128, 16], F32)
    rec_keys = [('xrp_pad', i) for i in range(2)] + [('xrp', i, t) for i in range(2) for t in range(NTT)] + \
        [(nm, i) for nm in ('xc', 'xcb', 'a2') for i in range(2)] + \
        [(nm, i, t) for nm in ('rr', 'ii', 'tt') for i in range(2) for t in range(2)] + ['hcar']
    assert off[0] <= attn_epi_start, (off[0], attn_epi_start)
    S.alias(attn_core_keys, rec_keys)
    wa_bd = bp[:, BP_WA:BP_WA + 1024].rearrange("p (m n) -> p m n", m=8)
    wi_bd = bp[:, BP_WI:BP_WI + 1024].rearrange("p (m n) -> p m n", m=8)
    for i in range(2):
        A('dve', lambda e, i=i: e.memset(xrp[i][:, 0:P0], 0.0), writes=[('xrp_pad', i)])
    rec_w = {}

    def rec_front(m, hf):
        u = (2 * m + hf) % 2
        T = U[u]
        xp = xrp[m % 2]
        cw = cp[:, CP_CONVW + 4 * m: CP_CONVW + 4 * m + 4]
        cb = cp[:, CP_CONVB + m: CP_CONVB + m + 1]
        xrk = [('xrp_pad', m % 2)] + [('xrp', m % 2, t) for t in range(2 * hf + 2)]
        b0 = P0 + hf * HT
        xc, xcb = T['xc'], T['xcb']
        st = {}

        def proj(t2):
            def f():
                if hf == 0 and t2 == 0:
                    rec_w[m] = (load_w(w_in_d[:, XR_OFF + m * 128: XR_OFF + (m + 1) * 128]),
                                load_w(w_in_d[:, YR_OFF + m * 128: YR_OFF + (m + 1) * 128]))
                (wxr, wxrk), _ = rec_w[m]
                tt = 2 * hf + t2
                pb = next_ps(list(range(8)))

                def mm(e):
                    for kc in range(8):
                        ins = e.matmul(PS[pb][:], wxr[:, kc, :], xT[:, kc, tt * 512:(tt + 1) * 512],
                                       start=(kc == 0), stop=(kc == 7))
                    return ins
                A('pe', mm, reads=[wxrk] + xT_keys(tt), writes=[psk(pb)])
                A('act', lambda e: e.copy(out=xp[:, P0 + tt * 512:P0 + (tt + 1) * 512], in_=PS[pb][:]),
                  reads=[psk(pb)], writes=[('xrp', m % 2, tt)])
            return f
        st['proj0'], st['proj1'] = proj(0), proj(1)
        st['conv0'] = lambda: A('dve', lambda e: e.tensor_scalar(out=xc[:], in0=xp[:, b0:b0 + HT], scalar1=cw[:, 3:4],
                                                                 scalar2=cb, op0=ALU.mult, op1=ALU.add),
                                reads=xrk + ['cp'], writes=[('xc', u)])
        for j in range(3):
            st['conv%d' % (j + 1)] = lambda j=j: A('dve', lambda e: e.scalar_tensor_tensor(
                out=xc[:], in0=xp[:, b0 - 3 + j:b0 - 3 + j + HT], scalar=cw[:, j:j + 1], in1=xc[:],
                op0=ALU.mult, op1=ALU.add), reads=xrk + ['cp', ('xc', u)], writes=[('xc', u)])
        st['xcb'] = lambda: A('act', lambda e: e.copy(out=xcb[:], in_=xc[:]), reads=[('xc', u)], writes=[('xcb', u)])

        def gate(wbd, dst, bcol, bkey, kname, t2):
            def f():
                pb = next_ps(list(range(8)))
                A('pe', lambda e: e.matmul(PS[pb][:], wbd[:, m, :], xcb[:, t2 * 512:(t2 + 1) * 512],
                                           start=True, stop=True),
                  reads=[('xcb', u), 'bp'], writes=[psk(pb)])
                A('act', lambda e: e.activation(out=dst[:, t2 * 512:(t2 + 1) * 512], in_=PS[pb][:], func=AF.Tanh,
                                                bias=sc[:, bcol + m:bcol + m + 1], scale=0.5),
                  reads=[psk(pb), bkey], writes=[(kname, u, t2)])
            return f
        for (wbd, dst, bcol, bkey, kname) in ((wa_bd, T['rr'], SC_HBA, 'sc_hba', 'rr'),
                                              (wi_bd, T['ii'], SC_HBI, 'sc_hbi', 'ii')):
            for t2 in range(2):
                st['g_%s%d' % (kname, t2)] = gate(wbd, dst, bcol, bkey, kname, t2)
        return st

    def rec_back(m, hf):
        u = (2 * m + hf) % 2
        T = U[u]
        rr, ii, a2, tt_, xc = T['rr'], T['ii'], T['a2'], T['tt'], T['xc']
        rk_all = [('rr', u, t) for t in range(2)]
        ik_all = [('ii', u, t) for t in range(2)]
        tk_all = [('tt', u, t) for t in range(2)]
        st = {}

        def expo(out, func, col):
            ap_ = sc[:, col + m:col + m + 1]
            return lambda: A('act', lambda e: e.activation(out=out[:], in_=rr[:], func=func, scale=ap_, bias=ap_),
                             reads=rk_all + CKEYS, writes=[('a2', u)] if out is a2 else (tk_all if out is tt_ else rk_all))
        st['a2'] = expo(a2, AF.Exp, SC_C16)
        st['T'] = expo(tt_, AF.Tanh, SC_C8)
        st['a'] = expo(rr, AF.Exp, SC_CM8)
        st['om'] = lambda: A('dve', lambda e: e.scalar_tensor_tensor(out=a2[:], in0=a2[:], scalar=1.0, in1=tt_[:],
                                                                     op0=ALU.add, op1=ALU.mult),
                             reads=[('a2', u)] + tk_all, writes=[('a2', u)])
        st['sqrt'] = lambda: A('act', lambda e: e.activation(out=a2[:], in_=a2[:], func=AF.Sqrt),
                               reads=[('a2', u)], writes=[('a2', u)])
        st['u1'] = lambda: A('dve', lambda e: e.scalar_tensor_tensor(out=a2[:], in0=ii[:], scalar=1.0, in1=a2[:],
                                                                     op0=ALU.add, op1=ALU.mult),
                             reads=[('a2', u)] + ik_all, writes=[('a2', u)])
        st['u2'] = lambda: A('dve', lambda e: e.scalar_tensor_tensor(out=a2[:], in0=a2[:], scalar=0.5, in1=xc[:],
                                                                     op0=ALU.mult, op1=ALU.mult),
                             reads=[('a2', u), ('xc', u)], writes=[('a2', u)])

        def scan():
            init = 0.0 if hf == 0 else hcar[:, m:m + 1]
            A('dve', lambda e: e.tensor_tensor_scan(out=ii[:], data0=rr[:], data1=a2[:], initial=init, op0=ALU.mult,
                                                    op1=ALU.add), reads=rk_all + [('a2', u)] + ik_all + ['hcar'],
              writes=ik_all)
            if hf == 0:
                A('dve', lambda e: e.tensor_copy(out=hcar[:, m:m + 1], in_=ii[:, HT - 1:HT]), reads=ik_all,
                  writes=['hcar'])
        st['scan'] = scan

        pby = {}

        def yr_mm(t2):
            def f():
                _, (wyr, wyrk) = rec_w[m]
                tt = 2 * hf + t2
                pb = next_ps(list(range(8)))
                pby[t2] = pb

                def mm(e):
                    for kc in range(8):
                        ins = e.matmul(PS[pb][:], wyr[:, kc, :], xT[:, kc, tt * 512:(tt + 1) * 512],
                                       start=(kc == 0), stop=(kc == 7))
                    return ins
                A('pe', mm, reads=[wyrk] + xT_keys(tt), writes=[psk(pb)])
            return f

        def yr_act(t2):
            def f():
                pb = pby[t2]
                A('act', lambda e: e.activation(out=tt_[:, t2 * 512:(t2 + 1) * 512], in_=PS[pb][:],
                                                func=AF.Gelu_apprx_tanh),
                  reads=[psk(pb)], writes=[('tt', u, t2)])
            return f
        st['yr0_mm'], st['yr1_mm'] = yr_mm(0), yr_mm(1)
        st['yr0_act'], st['yr1_act'] = yr_act(0), yr_act(1)
        st['mult'] = lambda: A('dve', lambda e: e.tensor_tensor(out=recT[:, m, hf * HT:(hf + 1) * HT], in0=ii[:],
                                                                in1=tt_[:], op=ALU.mult),
                               reads=ik_all + tk_all, writes=[('recT', m, hf)])
        return st

    ORDER = [('F', 'proj0'), ('F', 'proj1'), ('B', 'a2'), ('F', 'conv0'), ('B', 'T'), ('B', 'yr0_mm'), ('F', 'conv1'),
             ('B', 'a'), ('B', 'yr1_mm'),
             ('B', 'om'), ('F', 'conv2'), ('B', 'sqrt'), ('F', 'conv3'), ('F', 'xcb'), ('B', 'u1'), ('B', 'u2'),
             ('B', 'yr0_act'), ('F', 'g_rr0'), ('B', 'yr1_act'), ('F', 'g_rr1'), ('F', 'g_ii0'), ('F', 'g_ii1'),
             ('B', 'scan'), ('B', 'mult')]
    units = [(m, hf) for m in range(8) for hf in range(2)]
    prev_back = None
    for (m, hf) in units + [(None, None)]:
        fr = rec_front(m, hf) if m is not None else None
        for (w_, nm) in ORDER:
            d_ = fr if w_ == 'F' else prev_back
            if d_ is not None:
                d_[nm]()
        prev_back = rec_back(m, hf) if m is not None else None

    if dbg is not None and dbg[0] == 'recT':
        A('sp', lambda e: e.dma_start(out=dbg_d, in_=recT[:].rearrange("p c n -> p (c n)")),
          reads=[('recT', m, hf) for m in range(8) for hf in range(2)], writes=['dbg'], slot='dbg')
    if stop_after == 'recT':
        return finish([])

    walloc_reset()
    mergedT = walloc("mergedT", [128, 8, SEQ], BF16)
    gg = [walloc("gg%d" % i, [128, 512], F32) for i in range(4)]
    mm0 = [walloc("mm0_%d" % i, [128, 512], F32) for i in range(2)]
    merge_keys = [('mergedT', j, t) for j in range(8) for t in range(NTT)] + [('gg', i) for i in range(4)] + \
        [('mm0', i) for i in range(2)]
    S.alias(rec_keys, merge_keys)
    it = 0
    for j in range(8):
        wg0, wg0k = load_w(w_in_d[:, G_OFF + j * 128: G_OFF + (j + 1) * 128])
        wg1, wg1k = load_w(w_in_d[:, G_OFF + D + j * 128: G_OFF + D + (j + 1) * 128])
        wba, wbak = load_w(w_bra_d[:, j * 128:(j + 1) * 128])
        wbr, wbrk = load_w(w_brr_d[:, j * 128:(j + 1) * 128])
        for tt in range(NTT):
            pbs = [next_ps(list(range(8))) for _ in range(4)]
            attn_k = [('attnT', h, tt) for h in range(NH)]
            rec_k = [('recT', m, tt // 2) for m in range(8)]
            for (pb, wt, wk, src, sk) in ((pbs[0], wg0, wg0k, xT, xT_keys(tt)), (pbs[1], wg1, wg1k, xT, xT_keys(tt)),
                                          (pbs[2], wba, wbak, attnT, attn_k), (pbs[3], wbr, wbrk, recT, rec_k)):
                def mm(e, pb=pb, wt=wt, src=src, tt=tt):
                    for kc in range(8):
                        ins = e.matmul(PS[pb][:], wt[:, kc, :], src[:, kc, tt * 512:(tt + 1) * 512],
                                       start=(kc == 0), stop=(kc == 7))
                    return ins
                A('pe', mm, reads=[wk] + sk, writes=[psk(pb)])
            g0 = gg[(it % 2) * 2]
            g1 = gg[(it % 2) * 2 + 1]
            g0k = ('gg', (it % 2) * 2)
            g1k = ('gg', (it % 2) * 2 + 1)
            m0 = mm0[it % 2]
            m0k = ('mm0', it % 2)
            it += 1
            A('act', lambda e, g0=g0, pb=pbs[0], j=j: e.activation(out=g0[:], in_=PS[pb][:], func=AF.Sigmoid,
                                                                   bias=cp[:, CP_BGATE + j:CP_BGATE + j + 1]),
              reads=[psk(pbs[0]), 'cp'], writes=[g0k])
            A('act', lambda e, g1=g1, pb=pbs[1], j=j: e.activation(out=g1[:], in_=PS[pb][:], func=AF.Sigmoid,
                                                                   bias=cp[:, CP_BGATE + 8 + j:CP_BGATE + 8 + j + 1]),
              reads=[psk(pbs[1]), 'cp'], writes=[g1k])
            A('dve', lambda e, m0=m0, g0=g0, pb=pbs[2]: e.tensor_tensor(out=m0[:], in0=g0[:], in1=PS[pb][:], op=ALU.mult),
              reads=[g0k, psk(pbs[2])], writes=[m0k])
            A('dve', lambda e, g1=g1, pb=pbs[3]: e.tensor_tensor(out=g1[:], in0=g1[:], in1=PS[pb][:], op=ALU.mult),
              reads=[g1k, psk(pbs[3])], writes=[g1k])
            A('dve', lambda e, m0=m0, g1=g1, j=j, tt=tt: e.tensor_tensor(
                out=mergedT[:, j, tt * 512:(tt + 1) * 512], in0=m0[:], in1=g1[:], op=ALU.add),
              reads=[m0k, g1k], writes=[('mergedT', j, tt)])

    if dbg is not None and dbg[0] == 'mergedT':
        A('sp', lambda e: e.dma_start(out=dbg_d, in_=mergedT[:].rearrange("p c n -> p (c n)")),
          reads=[('mergedT', j, t) for j in range(8) for t in range(NTT)], writes=['dbg'], slot='dbg')
    if stop_after == 'mergedT':
        return finish([])

    off[0] = work0 + 8 * SEQ * 2 + 6 * 2048
    woutb = walloc("woutb", [128, 8, D], BF16)
    xin = [walloc("xin%d" % i, [128, D], F32) for i in range(2)]
    ybs = [walloc("yb%d" % i, [128, D], F32) for i in range(2)]
    x1bs = [walloc("x1b%d" % i, [128, D], BF16) for i in range(2)]
    stts = [walloc("stt%d" % i, [128, 2, 6], F32) for i in range(2)]
    mvs = [walloc("mv%d" % i, [128, 8], F32) for i in range(2)]

    def ln_keys(kp):
        return [('yb', kp, 0), ('yb', kp, 1), ('stt', kp, 0), ('stt', kp, 1), ('mv', kp), ('mv2', kp), ('mv3', kp)]
    p4_keys = [('woutb', 0), ('woutb', 1), ('xin', 0), ('xin', 1), ('x1b', 0), ('x1b', 1)] + ln_keys(0) + ln_keys(1)
    big_old = [('xT', t) for t in range(NTB)] + [('attnT', h, t) for h in range(NH) for t in range(NTT)] + \
        [('recT', m, hf) for m in range(8) for hf in range(2)]
    big_new = [('x1', t) for t in range(NTB)] + [('x1T', t) for t in range(NTB)]
    S.alias(big_old, big_new)
    S.alias(rec_keys + attn_work_keys, p4_keys)
    for n in range(2):
        A('pool', lambda e, n=n: e.dma_start(
            out=woutb[:, :, n * 512:(n + 1) * 512],
            in_=w_out_d[:, n * 512:(n + 1) * 512].rearrange("(c p) n -> p c n", p=128)),
          writes=[('woutb', n)], slot=('woutb', n))
    A('sp', lambda e: e.dma_start(out=lng[:], in_=lnp_d[0]), writes=['lng'], slot='lng')
    A('sp', lambda e: e.dma_start(out=lnb[:], in_=lnp_d[1]), writes=['lnb'], slot='lnb')
    wdn = alloc("wdn", [128, NF, D], BF16, at=work0)
    fs = [0, 6, 11, 16, 22]

    def load_wdn(i):
        A('pool', lambda e: e.dma_start(
            out=wdn[:, fs[i]:fs[i + 1], :],
            in_=w_down_d[fs[i] * 128:fs[i + 1] * 128, :].rearrange("(c p) n -> p c n", p=128)),
          writes=[('wdn', i)], slot=('wdn', i))
    S.alias(merge_keys[32:], [('wdn', 3)])
    load_wdn(3)

    def layer_norm_block(kp, dst_ap, dst_keys, yb, stt, mv):
        ykeys = [('yb', kp, 0), ('yb', kp, 1)]
        for n in range(2):
            A('dve', lambda e, n=n: e.bn_stats(out=stt[:, n, :], in_=yb[:, n * 512:(n + 1) * 512]),
              reads=[('yb', kp, n)], writes=[('stt', kp, n)])
        A('dve', lambda e: e.bn_aggr(out=mv[:, 0:2], in_=stt[:].rearrange("p a b -> p (a b)")),
          reads=[('stt', kp, 0), ('stt', kp, 1)], writes=[('mv', kp)])
        A('act', lambda e: e.activation(out=mv[:, 2:3], in_=mv[:, 1:2], func=AF.Ln, bias=LN_EPS),
          reads=[('mv', kp)], writes=[('mv2', kp)])
        A('act', lambda e: e.activation(out=mv[:, 2:3], in_=mv[:, 2:3], func=AF.Exp, scale=-0.5),
          reads=[('mv2', kp)], writes=[('mv2', kp)])
        A('dve', lambda e: e.scalar_tensor_tensor(out=yb[:], in0=yb[:], scalar=mv[:, 0:1], in1=lng[:],
                                                  op0=ALU.subtract, op1=ALU.mult),
          reads=ykeys + [('mv', kp), 'lng'], writes=ykeys)
        A('dve', lambda e: e.scalar_tensor_tensor(out=dst_ap, in0=yb[:], scalar=mv[:, 2:3], in1=lnb[:],
                                                  op0=ALU.mult, op1=ALU.add),
          reads=ykeys + [('mv2', kp), 'lnb'], writes=dst_keys)

    tail4 = None
    for t in range(NTB):
        xi = xin[t % 2]
        xik = ('xin', t % 2)
        A('sp', lambda e, xi=xi, t=t: e.dma_start(out=xi[:], in_=x_d[t * 128:(t + 1) * 128, :]),
          writes=[xik], slot=xik)
        for n in range(2):
            pb = next_ps(list(range(6)))

            def mm(e, t=t, n=n, pb=pb):
                for kc in range(8):
                    ins = e.matmul(PS[pb][:], mergedT[:, kc, t * 128:(t + 1) * 128],
                                   woutb[:, kc, n * 512:(n + 1) * 512], start=(kc == 0), stop=(kc == 7))
                return ins
            A('pe', mm, reads=[('mergedT', j, t // 4) for j in range(8)] + [('woutb', n)], writes=[psk(pb)])
            A('dve', lambda e, xi=xi, n=n, pb=pb, yb=ybs[t % 2]: e.scalar_tensor_tensor(
                out=yb[:, n * 512:(n + 1) * 512], in0=xi[:, n * 512:(n + 1) * 512], scalar=ALPHA, in1=PS[pb][:],
                op0=ALU.mult, op1=ALU.add), reads=[xik, psk(pb)], writes=[('yb', t % 2, n)])
        if tail4 is not None:
            tail4()

        def tail4(t=t):
            x1b = x1bs[t % 2]
            x1bk = ('x1b', t % 2)
            pb = 6 + (t % 2)
            pv = PS[pb][:].bitcast(BF16).rearrange("p (c n) -> p c n", c=8)

            def tr(e):
                for c in range(8):
                    ins = e.transpose(pv[:, c, :], x1b[:, c * 128:(c + 1) * 128], ident)
                return ins
            A('pe', tr, reads=[x1bk, 'bp'], writes=[psk(pb)])
            A('act', lambda e: e.copy(out=x1T[:, :, t * 128:(t + 1) * 128], in_=pv),
              reads=[psk(pb)], writes=[('x1T', t)])
        layer_norm_block(t % 2, x1[:, t, :], [('x1', t)], ybs[t % 2], stts[t % 2], mvs[t % 2])
        A('act', lambda e, t=t, x1b=x1bs[t % 2]: e.copy(out=x1b[:], in_=x1[:, t, :]), reads=[('x1', t)],
          writes=[('x1b', t % 2)])
    tail4()

    if dbg is not None and dbg[0] == 'x1':
        A('sp', lambda e: e.dma_start(out=dbg_d, in_=x1[:].rearrange("p c n -> p (c n)")),
          reads=[('x1', t) for t in range(NTB)], writes=['dbg'], slot='dbg')
    if stop_after == 'x1':
        return finish([])

    off[0] = work0 + NF * D * 2
    hidT = walloc("hidT", [128, NF, 512], BF16)
    sg = [walloc("sg%d" % i, [128, 512], F32) for i in range(2)]
    yb5 = [walloc("yb5_%d" % i, [128, D], F32) for i in range(2)]
    stt5 = [walloc("stt5_%d" % i, [128, 2, 6], F32) for i in range(2)]
    mv5 = [walloc("mv5_%d" % i, [128, 8], F32) for i in range(2)]
    p5_keys = [('wdn', i) for i in range(3)] + [('hidT', f) for f in range(NF)] + [('sg', 0), ('sg', 1)] + \
        ln_keys(2) + ln_keys(3)
    S.alias(merge_keys + p4_keys, p5_keys)
    A('sp', lambda e: e.dma_start(out=lng[:], in_=lnp_d[2]), writes=['lng'], slot='lng')
    A('sp', lambda e: e.dma_start(out=lnb[:], in_=lnp_d[3]), writes=['lnb'], slot='lnb')
    out_keys = []
    for tq in range(NTT):
        x1T_k = [('x1T', 4 * tq + i) for i in range(4)]
        for f in range(NF):
            wg, wgk = load_w(w_gate_d[:, f * 128:(f + 1) * 128])
            wu, wuk = load_w(w_up_d[:, f * 128:(f + 1) * 128])
            if tq == 0 and f % 4 == 3 and f // 4 < 3:
                load_wdn(f // 4)
            pg = next_ps(list(range(8)))
            pu = next_ps(list(range(8)))
            for (pb, wt, wk) in ((pg, wg, wgk), (pu, wu, wuk)):
                def mm(e, pb=pb, wt=wt, tq=tq):
                    for kc in range(8):
                        ins = e.matmul(PS[pb][:], wt[:, kc, :], x1T[:, kc, tq * 512:(tq + 1) * 512],
                                       start=(kc == 0), stop=(kc == 7))
                    return ins
                A('pe', mm, reads=[wk] + x1T_k, writes=[psk(pb)])
            s_ = sg[f % 2]
            sk = ('sg', f % 2)
            A('act', lambda e, s_=s_, pg=pg: e.activation(out=s_[:], in_=PS[pg][:], func=AF.Silu),
              reads=[psk(pg)], writes=[sk])
            A('dve', lambda e, s_=s_, pu=pu, f=f: e.tensor_tensor(out=hidT[:, f, :], in0=s_[:], in1=PS[pu][:],
                                                                  op=ALU.mult),
              reads=[sk, psk(pu)], writes=[('hidT', f)])
        for tb in range(4):
            t = 4 * tq + tb
            for n in range(2):
                pb = next_ps(list(range(8)))

                def mm(e, tb=tb, n=n, pb=pb):
                    for f in range(NF):
                        ins = e.matmul(PS[pb][:], hidT[:, f, tb * 128:(tb + 1) * 128],
                                       wdn[:, f, n * 512:(n + 1) * 512], start=(f == 0), stop=(f == NF - 1))
                    return ins
                A('pe', mm, reads=[('hidT', f) for f in range(NF)] + [('wdn', i) for i in range(4)],
                  writes=[psk(pb)])
                A('dve', lambda e, t=t, n=n, pb=pb, yb=yb5[t % 2]: e.scalar_tensor_tensor(
                    out=yb[:, n * 512:(n + 1) * 512], in0=x1[:, t, n * 512:(n + 1) * 512], scalar=ALPHA,
                    in1=PS[pb][:], op0=ALU.mult, op1=ALU.add), reads=[('x1', t), psk(pb)],
                  writes=[('yb', 2 + t % 2, n)])
            kp = 2 + t % 2
            layer_norm_block(kp, yb5[t % 2][:], [('yb', kp, 0), ('yb', kp, 1)], yb5[t % 2], stt5[t % 2], mv5[t % 2])
            ok = ('out', t)
            A('sp', lambda e, t=t, yb=yb5[t % 2]: e.dma_start(out=out_d[t * 128:(t + 1) * 128, :], in_=yb[:]),
              reads=[('yb', kp, 0), ('yb', kp, 1)], writes=[ok], slot=('ost', t % 2))
            out_keys.append(ok)

    return finish(out_keys)


def _pack_consts(inp):
    f32 = np.float32
    cp = np.zeros((128, CP_W), f32)
    p = np.arange(128, dtype=np.float64)
    for h in range(NH):
        slope = 2.0 ** (-8.0 * (h + 1) / NH)
        for j in range(-3, 17):
            cp[:, CP_ALIBI + h * 20 + j + 3] = (slope * (p - 128.0 * j)).astype(f32)
    cp[:, CP_BGATE:CP_BGATE + 16] = inp["b_gate"][0].reshape(16, 128).T
    cp[:, CP_CONVW:CP_CONVW + 32] = inp["conv_w"][0].reshape(4, 8, 128).transpose(2, 1, 0).reshape(128, 32)
    cp[:, CP_CONVB:CP_CONVB + 8] = inp["conv_b"][0].reshape(8, 128).T
    cp[:, CP_BA:CP_BA + 8] = inp["b_a"][0].reshape(8, 128).T
    cp[:, CP_BI:CP_BI + 8] = inp["b_i"][0].reshape(8, 128).T
    cp[:, CP_LRU:CP_LRU + 8] = inp["lru_lambda"][0].reshape(8, 128).T
    cp[:, CP_GSUB:CP_GSUB + 128] = np.broadcast_to(inp["subln_g"][0], (128, 128))
    for i, k in enumerate(("lambda_q1", "lambda_k1", "lambda_q2", "lambda_k2")):
        cp[:, CP_LAMV + 64 * i:CP_LAMV + 64 * (i + 1)] = np.broadcast_to(inp[k][0], (128, 64))
    bp = np.zeros((128, BP_W), f32)
    bp[:, BP_ID:BP_ID + 128] = np.eye(128, dtype=f32)
    bp[:, BP_TRI:BP_TRI + 128] = np.triu(np.ones((128, 128), f32))
    bp[:, BP_MASKB:BP_MASKB + 128] = np.tril(np.full((128, 128), -30000.0, f32), -1)
    for name, base in (("w_a", BP_WA), ("w_i", BP_WI)):
        w = inp[name][0]
        for m in range(8):
            for b in range(2):
                bp[b * 64:(b + 1) * 64, base + m * 128 + b * 64: base + m * 128 + (b + 1) * 64] = w[2 * m + b]
    lnp = np.stack([np.broadcast_to(inp[k][0], (128, D)) for k in ("ln1_g", "ln1_b", "ln2_g", "ln2_b")]).astype(f32)
    return cp, bp, np.ascontiguousarray(lnp)


_NC_CACHE = {}


def _get_nc(dbg=None):
    key = dbg
    if key not in _NC_CACHE:
        _NC_CACHE[key] = build_nc(dbg)
    return _NC_CACHE[key]


def kernel(**inputs):
    inp = {k: np.asarray(v) for k, v in inputs.items()}
    cp, bp, lnp = _pack_consts(inp)
    shared = {
        "w_in": np.ascontiguousarray(inp["w_in"][0]),
        "w_br_attn": np.ascontiguousarray(inp["w_br_attn"][0]),
        "w_br_rec": np.ascontiguousarray(inp["w_br_rec"][0]),
        "w_out": np.ascontiguousarray(inp["w_out"][0]),
        "w_gate": np.ascontiguousarray(inp["w_gate"][0]),
        "w_up": np.ascontiguousarray(inp["w_up"][0]),
        "w_down": np.ascontiguousarray(inp["w_down"][0]),
        "cpack": cp, "bpack": bp, "lnp": lnp,
    }
    nc = _get_nc()
    in_maps = []
    for c in range(8):
        m = dict(shared)
        m["x"] = np.ascontiguousarray(inp["x"][c])
        in_maps.append(m)
    res = run_bass_kernel_spmd(nc, in_maps, core_ids=list(range(8)))
    out = np.stack([np.asarray(r["out"]) for r in res.results], axis=0)
    return out.astype(np.float32)
```
